# Optimizing a Trainium2 kernel written in Bass

```python
import math
import jax, jax.numpy as jnp
from jax import lax
import numpy as np

D_MODEL = 1024
BATCH = 8
SEQ = 4096
DEPTH = 2

N_MIXERS = 2
N_A = (DEPTH + 1) // 2
N_B = DEPTH // 2
D_FF = 2816
LRU_WIDTH = D_MODEL
LRU_HEADS = 4
LRU_BLOCK = LRU_WIDTH // LRU_HEADS
CONV_WIDTH = 4
LRU_C = 8.0
HEAD_DIM = 64
N_Q_HEADS = D_MODEL // HEAD_DIM
N_KV_HEADS = 2
Q_PER_KV = N_Q_HEADS // N_KV_HEADS
WINDOW = 128
ATTN_BLOCK = WINDOW
QKV_WIDTH = (N_Q_HEADS + 2 * N_KV_HEADS) * HEAD_DIM
RMS_EPS = 1e-6
NEG_INF = -1e30

kernel_name = "hybrid_rglru_swa_sink_macaron"


def rms_norm(x, g):
    xf = x.astype(jnp.float32)
    y = xf * lax.rsqrt(jnp.mean(xf * xf, axis=-1, keepdims=True) + RMS_EPS)
    return (y * g.astype(jnp.float32)).astype(x.dtype)


def swiglu(x, w_gate, w_up, w_down):
    return (jax.nn.silu(x @ w_gate) * (x @ w_up)) @ w_down


def causal_depthwise_conv(x, w, b):
    s = x.shape[1]
    xp = jnp.pad(x, ((0, 0), (CONV_WIDTH - 1, 0), (0, 0)))
    out = b
    for k in range(CONV_WIDTH):
        out = out + xp[:, k:k + s] * w[k]
    return out


def block_diag_linear(x, w, b):
    bsz, s, _ = x.shape
    xh = x.reshape(bsz, s, LRU_HEADS, LRU_BLOCK)
    y = jnp.einsum('bshi,hij->bshj', xh, w).reshape(bsz, s, LRU_WIDTH)
    return y + b


def _lin_combine(c1, c2):
    a1, b1 = c1
    a2, b2 = c2
    return a1 * a2, a2 * b1 + b2


def rglru_mixer(x, w_in, b_in, conv_w, conv_b, w_a, b_a, w_i, b_i, lam, w_out, b_out):
    proj = x @ w_in + b_in
    y_branch = jax.nn.gelu(proj[..., :LRU_WIDTH])
    xb = causal_depthwise_conv(proj[..., LRU_WIDTH:], conv_w, conv_b)
    r = jax.nn.sigmoid(block_diag_linear(xb, w_a, b_a).astype(jnp.float32))
    i_gate = jax.nn.sigmoid(block_diag_linear(xb, w_i, b_i).astype(jnp.float32))
    log_a = LRU_C * r * jax.nn.log_sigmoid(lam.astype(jnp.float32))
    a = jnp.exp(log_a)
    mult = jnp.sqrt(-jnp.expm1(2.0 * log_a))
    u = mult * (i_gate * xb.astype(jnp.float32))
    _, h = lax.associative_scan(_lin_combine, (a, u), axis=1)
    return (h.astype(x.dtype) * y_branch) @ w_out + b_out


def swa_sink_attention(x, w_qkv, b_qkv, sinks, w_o, b_o):
    bsz, s, _ = x.shape
    nb = s // ATTN_BLOCK
    qkv = x @ w_qkv + b_qkv
    qd = N_Q_HEADS * HEAD_DIM
    kd = N_KV_HEADS * HEAD_DIM
    q = qkv[..., :qd].reshape(bsz, nb, ATTN_BLOCK, N_KV_HEADS, Q_PER_KV, HEAD_DIM)
    k = qkv[..., qd:qd + kd].reshape(bsz, nb, ATTN_BLOCK, N_KV_HEADS, HEAD_DIM)
    v = qkv[..., qd + kd:].reshape(bsz, nb, ATTN_BLOCK, N_KV_HEADS, HEAD_DIM)
    k_prev = jnp.concatenate([jnp.zeros_like(k[:, :1]), k[:, :-1]], axis=1)
    v_prev = jnp.concatenate([jnp.zeros_like(v[:, :1]), v[:, :-1]], axis=1)
    kk = jnp.concatenate([k_prev, k], axis=2)
    vv = jnp.concatenate([v_prev, v], axis=2)
    scale = 1.0 / math.sqrt(HEAD_DIM)
    scores = jnp.einsum('bnqkgd,bnskd->bnkgqs', q, kk).astype(jnp.float32) * scale
    qi = jnp.arange(ATTN_BLOCK)[:, None]
    kj = jnp.arange(2 * ATTN_BLOCK)[None, :]
    diff = qi + ATTN_BLOCK - kj
    band = (diff >= 0) & (diff < WINDOW)
    blk_idx = jnp.arange(nb)[:, None, None]
    valid = band[None] & ((blk_idx > 0) | (kj[None] >= ATTN_BLOCK))
    scores = jnp.where(valid[None, :, None, None], scores, NEG_INF)
    sink = jnp.broadcast_to(
        sinks.astype(jnp.float32).reshape(1, 1, N_KV_HEADS, Q_PER_KV, 1, 1),
        scores.shape[:-1] + (1,))
    probs = jax.nn.softmax(jnp.concatenate([scores, sink], axis=-1), axis=-1)[..., :-1]
    out = jnp.einsum('bnkgqs,bnskd->bnqkgd', probs.astype(x.dtype), vv)
    return out.reshape(bsz, s, qd) @ w_o + b_o


def setup_inputs(seed: int = 0) -> dict:
    key = jax.random.key(seed)
    ks = iter(jax.random.split(key, 40))
    f32 = jnp.float32

    def nrm(shape, fan_in):
        return jax.random.normal(next(ks), shape, f32) * (fan_in ** -0.5)

    def gain(shape):
        return 1.0 + 0.05 * jax.random.normal(next(ks), shape, f32)

    def bias(shape):
        return 0.01 * jax.random.normal(next(ks), shape, f32)

    x = jax.random.normal(next(ks), (BATCH, SEQ, D_MODEL), f32)
    norm_ffn = gain((DEPTH, 2, 2, D_MODEL))
    norm_mix = gain((DEPTH, 2, D_MODEL))
    ffn_w_gate = nrm((DEPTH, 2, D_MODEL, D_FF), D_MODEL)
    ffn_w_up = nrm((DEPTH, 2, D_MODEL, D_FF), D_MODEL)
    ffn_w_down = nrm((DEPTH, 2, D_FF, D_MODEL), D_FF)
    lru_w_in = nrm((N_A, D_MODEL, 2 * LRU_WIDTH), D_MODEL)
    lru_b_in = bias((N_A, 2 * LRU_WIDTH))
    lru_conv_w = nrm((N_A, CONV_WIDTH, LRU_WIDTH), CONV_WIDTH)
    lru_conv_b = bias((N_A, LRU_WIDTH))
    lru_w_a = nrm((N_A, LRU_HEADS, LRU_BLOCK, LRU_BLOCK), LRU_BLOCK)
    lru_b_a = bias((N_A, LRU_WIDTH))
    lru_w_i = nrm((N_A, LRU_HEADS, LRU_BLOCK, LRU_BLOCK), LRU_BLOCK)
    lru_b_i = bias((N_A, LRU_WIDTH))
    a0 = jax.random.uniform(next(ks), (N_A, LRU_WIDTH), f32, 0.81, 0.998)
    base = a0 ** (1.0 / LRU_C)
    lru_lambda = jnp.log(base) - jnp.log1p(-base)
    lru_w_out = nrm((N_A, LRU_WIDTH, D_MODEL), LRU_WIDTH)
    lru_b_out = bias((N_A, D_MODEL))
    attn_w_qkv = nrm((N_B, D_MODEL, QKV_WIDTH), D_MODEL)
    attn_b_qkv = bias((N_B, QKV_WIDTH))
    attn_sinks = jax.random.normal(next(ks), (N_B, N_Q_HEADS), f32)
    attn_w_o = nrm((N_B, N_Q_HEADS * HEAD_DIM, D_MODEL), N_Q_HEADS * HEAD_DIM)
    attn_b_o = bias((N_B, D_MODEL))
    return {"x": x, "norm_ffn": norm_ffn, "norm_mix": norm_mix,
            "ffn_w_gate": ffn_w_gate, "ffn_w_up": ffn_w_up, "ffn_w_down": ffn_w_down,
            "lru_w_in": lru_w_in, "lru_b_in": lru_b_in, "lru_conv_w": lru_conv_w,
            "lru_conv_b": lru_conv_b, "lru_w_a": lru_w_a, "lru_b_a": lru_b_a,
            "lru_w_i": lru_w_i, "lru_b_i": lru_b_i, "lru_lambda": lru_lambda,
            "lru_w_out": lru_w_out, "lru_b_out": lru_b_out,
            "attn_w_qkv": attn_w_qkv, "attn_b_qkv": attn_b_qkv, "attn_sinks": attn_sinks,
            "attn_w_o": attn_w_o, "attn_b_o": attn_b_o}


def reference(x, norm_ffn, norm_mix, ffn_w_gate, ffn_w_up, ffn_w_down,
              lru_w_in, lru_b_in, lru_conv_w, lru_conv_b, lru_w_a, lru_b_a,
              lru_w_i, lru_b_i, lru_lambda, lru_w_out, lru_b_out,
              attn_w_qkv, attn_b_qkv, attn_sinks, attn_w_o, attn_b_o):
    h = x
    for layer in range(DEPTH):
        f = swiglu(rms_norm(h, norm_ffn[layer, 0, 0]),
                   ffn_w_gate[layer, 0], ffn_w_up[layer, 0], ffn_w_down[layer, 0])
        h = h + 0.5 * rms_norm(f, norm_ffn[layer, 0, 1])
        hn = rms_norm(h, norm_mix[layer, 0])
        j = layer // N_MIXERS
        if layer % N_MIXERS == 0:
            m = rglru_mixer(hn, lru_w_in[j], lru_b_in[j], lru_conv_w[j], lru_conv_b[j],
                            lru_w_a[j], lru_b_a[j], lru_w_i[j], lru_b_i[j],
                            lru_lambda[j], lru_w_out[j], lru_b_out[j])
        else:
            m = swa_sink_attention(hn, attn_w_qkv[j], attn_b_qkv[j], attn_sinks[j],
                                   attn_w_o[j], attn_b_o[j])
        h = h + rms_norm(m, norm_mix[layer, 1])
        f = swiglu(rms_norm(h, norm_ffn[layer, 1, 0]),
                   ffn_w_gate[layer, 1], ffn_w_up[layer, 1], ffn_w_down[layer, 1])
        h = h + 0.5 * rms_norm(f, norm_ffn[layer, 1, 1])
    return h
```

```python
import numpy as np
from contextlib import ExitStack
import concourse.bass as bass
import concourse.mybir as mybir
from concourse.bass_utils import run_bass_kernel_spmd

F32 = mybir.dt.float32
BF16 = mybir.dt.bfloat16
AF = mybir.ActivationFunctionType
ALU = mybir.AluOpType
AX = mybir.AxisListType

D = 1024
SEQ = 4096
DFF = 2816
NCH = 8
NFF = 22
T = 512
NT = SEQ // T
NBLK = T // 128
EPS = 1e-6
NCORES = 8
NWS = 8
WCOLS = 2048
ALL_STAGES = ("ffn0", "lru", "ffn1", "ffn2", "attn", "ffn3")

NF, NM, BIN, CW, CB, BA, BI, LAM, BO, BQ, BK, AO, NPV = 0, 64, 96, 112, 144, 152, 160, 168, 176, 184, 192, 200, 208
WM_LIN, WM_LG, WM_LOUT, WM_AQ, WM_AKV, WM_AO = 0, 8, 12, 16, 20, 21


class Sem:
    def __init__(self, nc, es, name):
        self.h = es.enter_context(nc.semaphore(name))
        self.v = 0


class Res:
    __slots__ = ("lw", "rd")

    def __init__(self):
        self.lw = None
        self.rd = {}


class Builder:
    BASE = dict(pe=0.01, act=0.25, dve=0.12, pool=0.5, sp=0.1)
    RATE = dict(pe=2370.0, act=1400.0, dve=960.0, pool=300.0, sp=1e9)
    HOP = 0.3

    def __init__(self, nc, es):
        self.nc = nc
        self.es = es
        self.eng = dict(pe=nc.tensor, act=nc.scalar, dve=nc.vector, pool=nc.gpsimd, sp=nc.sync)
        self.esem = {k: Sem(nc, es, "s_" + k) for k in self.eng}
        self.waited = {k: {} for k in self.eng}
        self.res = {}
        self.etime = {k: 0.0 for k in self.eng}
        self.ttime = {}
        self.dma_free = 0.0
        self.cur = 0
        self.stime = [0.0, 0.0]
        self.acttab = None
        self.ntab = 0

    def R(self, *key):
        r = self.res.get(key)
        if r is None:
            r = self.res[key] = Res()
        return r

    def _wait(self, en, sem, val):
        w = self.waited[en]
        if w.get(sem, 0) >= val:
            return
        self.eng[en].wait_ge(sem.h, val)
        w[sem] = val

    def _deps(self, en, reads, writes):
        need = {}
        for r in reads:
            if r.lw is not None:
                s, v = r.lw
                if need.get(s, 0) < v:
                    need[s] = v
        for r in writes:
            if r.lw is not None:
                s, v = r.lw
                if need.get(s, 0) < v:
                    need[s] = v
            for s, v in r.rd.items():
                if need.get(s, 0) < v:
                    need[s] = v
        ready = 0.0
        for s, v in need.items():
            self._wait(en, s, v)
            t = self.ttime.get((s, v), 0.0)
            if t > ready:
                ready = t
        return ready

    def _reg(self, tok, reads, writes):
        s, v = tok
        for r in writes:
            r.lw = tok
            r.rd = {}
        for r in reads:
            if r.rd.get(s, 0) < v:
                r.rd[s] = v

    def _time(self, en, ready, dur, tok):
        start = max(self.etime[en], ready + self.HOP)
        end = start + dur
        self.etime[en] = end
        self.ttime[tok] = end
        if end > self.stime[self.cur]:
            self.stime[self.cur] = end

    def op(self, en, fn, reads=(), writes=(), n=512, f=1.0, tab=None):
        ready = self._deps(en, reads, writes)
        if tab is not None and tab != self.acttab:
            self.acttab = tab
            self.etime[en] = max(self.etime[en], ready) + 1.28
            self.ntab += 1
        ins = fn(self.eng[en])
        sem = self.esem[en]
        sem.v += 1
        ins.then_inc(sem.h, 1)
        tok = (sem, sem.v)
        self._reg(tok, reads, writes)
        feff = 1.0 + 0.35 * (f - 1.0) if f <= 2.0 else f
        self._time(en, ready, self.BASE[en] + feff * n / self.RATE[en], tok)
        return tok

    def dma(self, en, sem, out, in_, reads=(), writes=(), nbytes=1 << 20, accum=False):
        ready = self._deps(en, reads, writes)
        if accum:
            ins = self.eng[en].dma_start(out=out, in_=in_, accum_op=ALU.add)
        else:
            ins = self.eng[en].dma_start(out=out, in_=in_)
        sem.v += 16
        ins.then_inc(sem.h, 16)
        tok = (sem, sem.v)
        self._reg(tok, reads, writes)
        if en == "pool":
            self.ttime[tok] = 0.0
            return tok
        issue_end = max(self.etime[en], ready + self.HOP) + 0.1
        self.etime[en] = issue_end
        xs = max(issue_end, self.dma_free)
        self.dma_free = xs + nbytes / 300e3
        self.ttime[tok] = self.dma_free + 2.0
        return tok

    def mm_group(self, out_ap, pairs, reads, writes):
        ready = self._deps("pe", reads, writes)
        n = len(pairs)
        ins = None
        dur = 0.0
        for i, (l, r) in enumerate(pairs):
            ins = self.nc.tensor.matmul(out_ap, lhsT=l, rhs=r, start=(i == 0), stop=(i == n - 1))
            dur += max(r.free_size(), 96) / self.RATE["pe"] + 0.005
        sem = self.esem["pe"]
        sem.v += 1
        ins.then_inc(sem.h, 1)
        tok = (sem, sem.v)
        self._reg(tok, reads, writes)
        self._time("pe", ready, dur, tok)
        return tok

    def pe_manual(self, emit, reads, writes, dur):
        ready = self._deps("pe", reads, writes)
        ins = emit()
        sem = self.esem["pe"]
        sem.v += 1
        ins.then_inc(sem.h, 1)
        tok = (sem, sem.v)
        self._reg(tok, reads, writes)
        self._time("pe", ready, dur, tok)
        return tok


def build_program(stages=ALL_STAGES, ntiles=NT, plan_keys=None, offset=255.0, gy=2, dy=1, ay=1):
    if plan_keys is None:
        rec = []
        build_program(stages, ntiles, plan_keys=rec, offset=offset, gy=gy, dy=dy, ay=ay)
        plan_keys = tuple(rec)
        recording = False
    else:
        recording = isinstance(plan_keys, list)
    nc = bass.Bass("TRN2", target_bir_lowering=False)
    xT = nc.dram_tensor("xT", [D, SEQ], F32, kind="ExternalInput").ap()
    wgu = nc.dram_tensor("wgu", [4 * NFF, 128, 2048], F32, kind="ExternalInput").ap()
    wdn = nc.dram_tensor("wdn", [4 * 16, 128, 1408], F32, kind="ExternalInput").ap()
    wmix = nc.dram_tensor("wmix", [25, 128, 2048], F32, kind="ExternalInput").ap()
    pvec = nc.dram_tensor("pvec", [128, NPV], F32, kind="ExternalInput").ap()
    bcast = nc.dram_tensor("bcast", [128, 144], F32, kind="ExternalInput").ap()
    outT = nc.dram_tensor("outT", [D, SEQ], F32, kind="ExternalOutput").ap()
    xT3 = xT.rearrange("(kc p) t -> p kc t", p=128)
    outT3 = outT.rearrange("(kc p) t -> p kc t", p=128)

    with ExitStack() as es:
        def sb(name, shape, dt):
            return es.enter_context(nc.sbuf_tensor(name, shape, dt))

        NS_ = 2
        X = [sb(f"X{s}", [128, NCH, T], F32) for s in range(NS_)]
        XN = [sb(f"XN{s}", [128, NCH, T], BF16) for s in range(NS_)]
        BIG = [sb(f"BIG{s}", [128, 5632], F32) for s in range(NS_)]
        HY = [sb(f"HY{s}", [128, NCH, T], BF16) for s in range(NS_)]
        FB = [sb(f"FB{s}", [128, NCH, T], F32) for s in range(NS_)]
        SQ = [sb(f"SQ{s}", [128, NCH, T], BF16) for s in range(NS_)]
        RT = [sb(f"RT{s}", [128, T], F32) for s in range(NS_)]
        RS = [sb(f"RS{s}", [128, T], F32) for s in range(NS_)]
        SG = [sb(f"SG{s}", [128, 2, T], BF16) for s in range(NS_)]
        ST = [sb(f"ST{s}", [128, 3, 16], F32) for s in range(NS_)]
        W = sb("W", [128, NWS, WCOLS], BF16)
        PV = sb("PV", [128, NPV], F32)
        BC = sb("BC", [128, 144], F32)
        CG = sb("CG", [128, 64], F32)
        CL = sb("CL", [128, 24], F32)
        NSINK = sb("NSINK", [128, 16], F32)
        NSMAX = sb("NSMAX", [128, 1], F32)
        ONES = sb("ONES", [128, 128], BF16)
        IDENT = sb("IDENT", [128, 128], BF16)
        MASK = sb("MASK", [128, 256], BF16)
        MASK0 = sb("MASK0", [128, 256], BF16)
        MTMP = sb("MTMP", [128, 256], F32)
        HST = sb("HST", [128, 8], F32)
        CARRY = sb("CARRY", [128, 8, 4], F32)
        KC = sb("KC", [128, 128], BF16)
        VC = sb("VC", [128, 2, 128], BF16)
        EPSC = sb("EPSC", [128, 1], F32)
        ONEC = sb("ONEC", [128, 1], F32)
        PSALL = es.enter_context(nc.psum_tensor("psall", [128, 8 * 512], F32))
        PS = [PSALL[:, i * 512:(i + 1) * 512] for i in range(8)]

        B = Builder(nc, es)
        R = B.R
        wsem = [Sem(nc, es, f"w{i}") for i in range(NWS)]
        xsem = [[Sem(nc, es, f"xl{s}_{i}") for i in range(NCH)] for s in range(NS_)]
        osem = [[Sem(nc, es, f"xo{s}_{i}") for i in range(NCH)] for s in range(NS_)]
        asem = [[Sem(nc, es, f"xa{s}_{i}") for i in range(NCH)] for s in range(NS_)]
        csem = Sem(nc, es, "cst")
        bank_i = [0, 0]

        def nb(s):
            b = 4 * s + bank_i[s] % 4
            bank_i[s] += 1
            return b

        def nb2(s):
            if bank_i[s] % 2:
                bank_i[s] += 1
            b = 4 * s + bank_i[s] % 4
            bank_i[s] += 2
            return b

        def piece_src(key):
            kind = key[0]
            if kind == "gu":
                return wgu[key[2] * NFF + key[3]], 2048
            if kind == "d":
                return wdn[key[2] * 16 + key[3] * 2 + key[4]], 1408
            if kind == "lin":
                return wmix[WM_LIN + key[2]], 2048
            if kind == "lg":
                return wmix[WM_LG + key[2]], 1024
            if kind == "lout":
                return wmix[WM_LOUT + key[2]], 2048
            if kind == "aq":
                return wmix[WM_AQ + key[2]], 2048
            if kind == "akv":
                return wmix[WM_AKV], 2048
            if kind == "ao":
                return wmix[WM_AO + key[2]], 2048
            raise KeyError(key)

        wstate = dict(issued=0, acq=0)
        released = []

        def issue_one():
            i = wstate["issued"]
            slot = i % NWS
            src, ncols = piece_src(plan_keys[i])
            B.dma("pool", wsem[slot], W[:, slot, 0:ncols], src[:, 0:ncols], writes=[R("W", slot)],
                  nbytes=128 * ncols * 4)
            wstate["issued"] += 1

        def issue():
            if recording:
                return
            while wstate["issued"] < len(plan_keys):
                i = wstate["issued"]
                if i >= NWS and not (i - NWS < len(released) and released[i - NWS]):
                    break
                issue_one()

        def acq(key):
            i = wstate["acq"]
            wstate["acq"] += 1
            released.append(False)
            if recording:
                plan_keys.append(key)
                assert i < NWS or released[i - NWS], "too many weight pieces open"
                issue_one()
            else:
                assert plan_keys[i] == key, (plan_keys[i], key)
                assert i < wstate["issued"], "weight piece not prefetched (too many open)"
            return i

        def rel(i):
            released[i] = True
            issue()

        def ws(i):
            return i % NWS

        B.dma("sp", csem, PV[:, :], pvec[:, :], writes=[R("PV")])
        B.dma("sp", csem, BC[:, :], bcast[:, :], writes=[R("BC")])
        for en in ("act", "dve", "pool"):
            B._wait(en, csem, csem.v)
        B.op("dve", lambda e: e.memset(ONES[:, :], 1.0), writes=[R("ONES")])
        B.op("dve", lambda e: e.memset(HST[:, :], 0.0), writes=[R("HST", c) for c in range(8)])
        B.op("dve", lambda e: e.memset(CARRY[:, :, :], 0.0), writes=[R("CARRY")])
        B.op("dve", lambda e: e.memset(KC[:, :], 0.0), writes=[R("KC")])
        B.op("dve", lambda e: e.memset(VC[:, :, :], 0.0), writes=[R("VC")])
        B.op("dve", lambda e: e.memset(MTMP[:, :], 0.0), writes=[R("MTMP")])
        B.op("dve", lambda e: e.memset(EPSC[:, :], EPS), writes=[R("EPSC")])
        B.op("dve", lambda e: e.memset(ONEC[:, :], 1.0), writes=[R("ONEC")])
        B.op("dve", lambda e: e.tensor_scalar(out=CG[:, :], in0=PV[:, NF:NF + 64], scalar1=0.5, scalar2=None,
                                              op0=ALU.mult), reads=[R("PV")], writes=[R("CG")])
        B.op("dve", lambda e: e.tensor_scalar(out=NSINK[:, :], in0=BC[:, 128:144], scalar1=-1.0, scalar2=None,
                                              op0=ALU.mult), reads=[R("BC")], writes=[R("NSINK")])
        B.op("dve", lambda e: e.tensor_reduce(out=NSMAX[:, :], in_=NSINK[:, :], axis=AX.X, op=ALU.min),
             reads=[R("NSINK")], writes=[R("NSMAX")])
        B.op("act", lambda e: e.activation(out=CL[:, 0:8], in_=PV[:, LAM:LAM + 8], func=AF.Exp, scale=-1.0),
             reads=[R("PV")], writes=[R("CL")], tab="exp")
        B.op("act", lambda e: e.activation(out=CL[:, 0:8], in_=CL[:, 0:8], func=AF.Ln, bias=1.0, scale=1.0),
             reads=[R("CL")], writes=[R("CL")], tab="exp")
        B.op("dve", lambda e: e.tensor_scalar(out=CL[:, 8:16], in0=CL[:, 0:8], scalar1=-8.0, scalar2=None,
                                              op0=ALU.mult), reads=[R("CL")], writes=[R("CL")])
        B.op("dve", lambda e: e.tensor_scalar(out=CL[:, 16:24], in0=CL[:, 0:8], scalar1=-16.0, scalar2=None,
                                              op0=ALU.mult), reads=[R("CL")], writes=[R("CL")])
        B.op("pool", lambda e: e.affine_select(out=IDENT[:, :], in_=ONES[:, :], pattern=[[-1, 128]],
                                               compare_op=ALU.is_equal, fill=0.0, base=0, channel_multiplier=1),
             reads=[R("ONES")], writes=[R("IDENT")])
        B.op("pool", lambda e: e.affine_select(out=MTMP[:, :], in_=MTMP[:, :], pattern=[[1, 256]],
                                               compare_op=ALU.is_ge, fill=-30000.0, base=-1, channel_multiplier=-1),
             reads=[R("MTMP")], writes=[R("MTMP")])
        B.op("pool", lambda e: e.affine_select(out=MASK[:, :], in_=MTMP[:, :], pattern=[[-1, 256]],
                                               compare_op=ALU.is_ge, fill=-30000.0, base=128, channel_multiplier=1),
             reads=[R("MTMP")], writes=[R("MASK")])
        B.op("pool", lambda e: e.affine_select(out=MASK0[:, :], in_=MASK[:, :], pattern=[[1, 256]],
                                               compare_op=ALU.is_ge, fill=-30000.0, base=-128, channel_multiplier=0),
             reads=[R("MASK")], writes=[R("MASK0")])
        for en in ("act", "pe"):
            B._wait(en, B.esem["dve"], B.esem["dve"].v)
        B._wait("pe", B.esem["pool"], B.esem["pool"].v)

        def norm_stats(s):
            b = nb(s)
            B.mm_group(PS[b], [(ONES[:, :], SQ[s][:, kc, :]) for kc in range(NCH)],
                       reads=[R("SQ", s, kc) for kc in range(NCH)] + [R("ONES")], writes=[R("PS", b)])
            B.op("act", lambda e: e.activation(out=RT[s][:, :], in_=PS[b], func=AF.Ln,
                                               bias=EPSC[:, 0:1], scale=1.0 / D),
                 reads=[R("PS", b)], writes=[R("RT", s)], tab="exp")
            B.op("act", lambda e: e.activation(out=PS[b], in_=RT[s][:, :], func=AF.Exp, scale=-0.5),
                 reads=[R("RT", s)], writes=[R("PS", b)], tab="exp")
            return b

        sq_ready = [False, False]

        def norm_in(s, gcol):
            if not sq_ready[s]:
                for kc in range(NCH):
                    B.op("act", lambda e, kc=kc: e.activation(out=SQ[s][:, kc, :], in_=X[s][:, kc, :], func=AF.Square),
                         reads=[R("X", s, kc)], writes=[R("SQ", s, kc)])
            sq_ready[s] = False
            b = norm_stats(s)
            for kc in range(NCH):
                B.op("dve", lambda e, kc=kc: e.scalar_tensor_tensor(
                    out=XN[s][:, kc, :], in0=X[s][:, kc, :], scalar=PV[:, gcol + kc:gcol + kc + 1],
                    in1=PS[b], op0=ALU.mult, op1=ALU.mult),
                    reads=[R("X", s, kc), R("PS", b), R("PV")], writes=[R("XN", s, kc)], f=1.0)

        def norm_out(s, gt, gcol, presq, final=None):
            b = norm_stats(s)

            def mult(i):
                B.op("dve", lambda e: e.tensor_tensor(out=FB[s][:, i, :], in0=FB[s][:, i, :],
                                                      in1=PS[b], op=ALU.mult),
                     reads=[R("FB", s, i), R("PS", b)], writes=[R("FB", s, i)], f=1.0)

            if final is None:
                mult(0)
            for i in range(NCH):
                if final is None and i + 1 < NCH:
                    mult(i + 1)
                if final is not None:
                    B.op("dve", lambda e, i=i: e.scalar_tensor_tensor(
                        out=FB[s][:, i, :], in0=FB[s][:, i, :], scalar=gt[:, gcol + i:gcol + i + 1],
                        in1=PS[b], op0=ALU.mult, op1=ALU.mult),
                        reads=[R("FB", s, i), R("PS", b), R("PV"), R("CG")], writes=[R("FB", s, i)], f=1.0)
                    B.dma("pool", asem[s][i], outT3[:, i, final * T:(final + 1) * T], FB[s][:, i, :],
                          reads=[R("FB", s, i)], writes=[R("OUT", s, i)], nbytes=1 << 18, accum=True)
                    continue
                B.op("dve", lambda e, i=i: e.scalar_tensor_tensor(
                    out=X[s][:, i, :], in0=FB[s][:, i, :], scalar=gt[:, gcol + i:gcol + i + 1],
                    in1=X[s][:, i, :], op0=ALU.mult, op1=ALU.add),
                    reads=[R("FB", s, i), R("X", s, i), R("PV"), R("CG")], writes=[R("X", s, i)], f=2.0)
                if presq:
                    B.op("act", lambda e, i=i: e.activation(out=SQ[s][:, i, :], in_=X[s][:, i, :], func=AF.Square),
                         reads=[R("X", s, i)], writes=[R("SQ", s, i)])
                if i % 2 == 1:
                    yield 1.0
            sq_ready[s] = presq

        xn_ready = [False, False]

        def early_io(s, k):
            for i in range(NCH):
                B.dma("sp", osem[s][i], outT3[:, i, k * T:(k + 1) * T], X[s][:, i, :],
                      reads=[R("X", s, i)], writes=[R("OUT", s, i)], nbytes=1 << 18)
            if k + NS_ < ntiles:
                for i in range(NCH):
                    B.dma("sp", xsem[s][i], X[s][:, i, :], xT3[:, i, (k + NS_) * T:(k + NS_ + 1) * T],
                          writes=[R("X", s, i)], nbytes=1 << 18)

        def early_stats(s):
            for kc in range(NCH):
                B.op("act", lambda e, kc=kc: e.activation(out=HY[s][:, kc, :], in_=X[s][:, kc, :], func=AF.Square),
                     reads=[R("X", s, kc)], writes=[R("HY", s, kc)])
            b = nb(s)
            B.mm_group(PS[b], [(ONES[:, :], HY[s][:, kc, :]) for kc in range(NCH)],
                       reads=[R("HY", s, kc) for kc in range(NCH)] + [R("ONES")], writes=[R("PS", b)])
            B.op("act", lambda e: e.activation(out=RT[s][:, :], in_=PS[b], func=AF.Ln,
                                               bias=EPSC[:, 0:1], scale=1.0 / D),
                 reads=[R("PS", b)], writes=[R("RT", s)], tab="exp")
            B.op("act", lambda e: e.activation(out=RT[s][:, :], in_=RT[s][:, :], func=AF.Exp, scale=-0.5),
                 reads=[R("RT", s)], writes=[R("RT", s)], tab="exp")

        def early_xn(s, gcol):
            for kc in range(NCH):
                B.op("dve", lambda e, kc=kc: e.scalar_tensor_tensor(
                    out=XN[s][:, kc, :], in0=X[s][:, kc, :], scalar=PV[:, gcol + kc:gcol + kc + 1],
                    in1=RT[s][:, :], op0=ALU.mult, op1=ALU.mult),
                    reads=[R("X", s, kc), R("RT", s), R("PV")], writes=[R("XN", s, kc)], f=2.0)
            xn_ready[s] = True

        def out_proj(s, k, kind, bcol, gt, gcol, presq, final=None):
            for ip in range(4):
                pi = acq((kind, k, ip))
                slot = ws(pi)
                for i2 in range(2):
                    i = 2 * ip + i2
                    bf = nb(s)
                    B.mm_group(PS[bf],
                               [(W[:, slot, (i2 * 8 + kc) * 128:(i2 * 8 + kc + 1) * 128], HY[s][:, kc, :])
                                for kc in range(NCH)],
                               reads=[R("W", slot)] + [R("HY", s, kc) for kc in range(NCH)],
                               writes=[R("PS", bf)])
                    B.op("act", lambda e: e.activation(
                        out=FB[s][:, i, :], in_=PS[bf], func=AF.Identity,
                        bias=PV[:, bcol + i:bcol + i + 1], scale=1.0),
                        reads=[R("PS", bf), R("PV")], writes=[R("FB", s, i)])
                    B.op("act", lambda e: e.activation(
                        out=SQ[s][:, i, :], in_=FB[s][:, i, :], func=AF.Square),
                        reads=[R("FB", s, i)], writes=[R("SQ", s, i)])
                rel(pi)
                yield 3.8
            yield from norm_out(s, gt, gcol, presq, final)

        sgi = [0, 0]

        def ffn(s, k, fi, presq, final=None):
            gin = NF + (fi * 2 + 0) * 8
            gout = (fi * 2 + 1) * 8
            A = BIG[s][:, :].bitcast(BF16).rearrange("p (j t) -> p j t", j=NFF)
            if xn_ready[s]:
                xn_ready[s] = False
            else:
                norm_in(s, gin)
            yield 10.0
            early = (final is not None and final + NS_ < ntiles and stages[0].startswith("ffn"))
            if final is not None:
                early_io(s, final)
            for j in range(NFF):
                if early and j == 12:
                    early_stats(s)
                pi = acq(("gu", k, fi, j))
                slot = ws(pi)
                bg = nb(s)
                bu = nb(s)
                xr = [R("XN", s, kc) for kc in range(NCH)]
                B.mm_group(PS[bg], [(W[:, slot, kc * 128:(kc + 1) * 128], XN[s][:, kc, :]) for kc in range(NCH)],
                           reads=[R("W", slot)] + xr, writes=[R("PS", bg)])
                B.mm_group(PS[bu], [(W[:, slot, 1024 + kc * 128:1024 + (kc + 1) * 128], XN[s][:, kc, :])
                                    for kc in range(NCH)],
                           reads=[R("W", slot)] + xr, writes=[R("PS", bu)])
                rel(pi)
                g = sgi[s] % 2
                sgi[s] += 1
                B.op("act", lambda e: e.activation(out=SG[s][:, g, :], in_=PS[bg], func=AF.Silu),
                     reads=[R("PS", bg)], writes=[R("SG", s, g)], tab="silu")
                B.op("dve", lambda e: e.tensor_tensor(out=A[:, j, :], in0=SG[s][:, g, :], in1=PS[bu], op=ALU.mult),
                     reads=[R("SG", s, g), R("PS", bu)], writes=[R("BIG", s, j)])
                if j % gy == gy - 1:
                    yield 7.6
            if early:
                early_xn(s, NF + (int(stages[0][3]) * 2 + 0) * 8)
                yield 6.0
            for i in range(NCH):
                p0 = acq(("d", k, fi, i, 0))
                p1 = acq(("d", k, fi, i, 1))
                bf = nb(s)
                pairs = []
                for f in range(NFF):
                    s_ = ws(p0) if f < 11 else ws(p1)
                    pairs.append((W[:, s_, (f % 11) * 128:(f % 11 + 1) * 128], A[:, f, :]))
                B.mm_group(PS[bf], pairs,
                           reads=[R("W", ws(p0)), R("W", ws(p1))] + [R("BIG", s, f) for f in range(NFF)],
                           writes=[R("PS", bf)])
                rel(p0)
                rel(p1)
                B.op("act", lambda e: e.activation(out=FB[s][:, i, :], in_=PS[bf], func=AF.Copy),
                     reads=[R("PS", bf)], writes=[R("FB", s, i)])
                B.op("act", lambda e: e.activation(out=SQ[s][:, i, :], in_=PS[bf], func=AF.Square),
                     reads=[R("PS", bf)], writes=[R("SQ", s, i)])
                if i % dy == dy - 1:
                    yield 5.2
            yield from norm_out(s, CG, gout, presq, final)

        def bigu(s, lo, hi):
            return [R("BIG", s, u) for u in range(lo // 256, (hi - 1) // 256 + 1)]

        lru_done = [False] * (ntiles + 1)
        akv_done = [False] * (ntiles + 1)

        def lru(s, k, presq, final=None):
            Ys = [BIG[s][:, hp * 1024:(hp + 1) * 1024].rearrange("p (c t) -> p c t", c=2) for hp in range(2)]
            PREs = [BIG[s][:, 2048 + hp * 1280:2048 + hp * 1280 + 1040].rearrange("p (c t) -> p c t", c=2)
                    for hp in range(2)]
            XBv = BIG[s][:, 4608:5632].rearrange("p (c t) -> p c t", c=2)
            XBb = SG[s]
            rYs = [bigu(s, hp * 1024, (hp + 1) * 1024) for hp in range(2)]
            rPREs = [bigu(s, 2048 + hp * 1280, 2048 + (hp + 1) * 1280) for hp in range(2)]
            rXB = bigu(s, 4608, 5632)
            rXBb = [R("SG", s, 0), R("SG", s, 1)]
            norm_in(s, NM + 0)
            yield 10.0
            if final is not None:
                early_io(s, final)
            while k > 0 and not lru_done[k - 1]:
                yield None

            def inproj(hd):
                Yv, PREv, rY, rPRE = Ys[hd % 2], PREs[hd % 2], rYs[hd % 2], rPREs[hd % 2]
                for cc in range(2):
                    c = 2 * hd + cc
                    pi = acq(("lin", k, c))
                    slot = ws(pi)
                    by = nb(s)
                    bp = nb(s)
                    xr = [R("XN", s, kc) for kc in range(NCH)]
                    B.mm_group(PS[by], [(W[:, slot, kc * 128:(kc + 1) * 128], XN[s][:, kc, :]) for kc in range(NCH)],
                               reads=[R("W", slot)] + xr, writes=[R("PS", by)])
                    B.mm_group(PS[bp], [(W[:, slot, 1024 + kc * 128:1024 + (kc + 1) * 128], XN[s][:, kc, :])
                                        for kc in range(NCH)],
                               reads=[R("W", slot)] + xr, writes=[R("PS", bp)])
                    rel(pi)
                    B.op("act", lambda e: e.activation(
                        out=Yv[:, cc, :], in_=PS[by], func=AF.Gelu_apprx_tanh,
                        bias=PV[:, BIN + c:BIN + c + 1], scale=1.0),
                        reads=[R("PS", by), R("PV")], writes=rY, tab="gelu")
                    B.op("act", lambda e: e.activation(
                        out=PREv[:, cc, 3:3 + T], in_=PS[bp], func=AF.Identity,
                        bias=PV[:, BIN + 8 + c:BIN + 8 + c + 1], scale=1.0),
                        reads=[R("PS", bp), R("PV")], writes=rPRE)
                    yield 3.8

            yield from inproj(0)
            for hd in range(4):
                Yv, PREv, rY, rPRE = Ys[hd % 2], PREs[hd % 2], rYs[hd % 2], rPREs[hd % 2]
                if hd + 1 < 4:
                    yield from inproj(hd + 1)
                for cc in range(2):
                    c = 2 * hd + cc
                    B.op("dve", lambda e, cc=cc, c=c: e.tensor_copy(out=PREv[:, cc, 0:3], in_=CARRY[:, c, 0:3]),
                         reads=[R("CARRY")], writes=rPRE, n=4)
                    B.op("act", lambda e, cc=cc, c=c: e.activation(
                        out=XBv[:, cc, :], in_=PREv[:, cc, 3:3 + T], func=AF.Identity,
                        scale=PV[:, CW + 3 * 8 + c:CW + 3 * 8 + c + 1], bias=PV[:, CB + c:CB + c + 1]),
                        reads=rPRE + [R("PV")], writes=[R("XBc", s, cc)] + bigu(s, 4608 + cc * 512, 5120 + cc * 512))
                for kk in range(3):
                    for cc in range(2):
                        c = 2 * hd + cc
                        B.op("dve", lambda e, kk=kk, cc=cc, c=c: e.scalar_tensor_tensor(
                            out=XBv[:, cc, :], in0=PREv[:, cc, kk:kk + T],
                            scalar=PV[:, CW + kk * 8 + c:CW + kk * 8 + c + 1],
                            in1=XBv[:, cc, :], op0=ALU.mult, op1=ALU.add),
                            reads=rPRE + [R("XBc", s, cc), R("PV")], writes=[R("XBc", s, cc)], f=2.0)
                for cc in range(2):
                    c = 2 * hd + cc
                    B.op("dve", lambda e, cc=cc, c=c: e.tensor_copy(out=CARRY[:, c, 0:3], in_=PREv[:, cc, T:T + 3]),
                         reads=rPRE, writes=[R("CARRY")], n=4)
                    B.op("act", lambda e, cc=cc: e.activation(out=XBb[:, cc, :], in_=XBv[:, cc, :], func=AF.Copy),
                         reads=[R("XBc", s, cc)] + bigu(s, 4608 + cc * 512, 5120 + cc * 512), writes=[rXBb[cc]])
                yield 8.0
                pi = acq(("lg", k, hd))
                slot = ws(pi)
                for oc in range(2):
                    c = 2 * hd + oc
                    br = nb(s)
                    bi = nb(s)
                    B.mm_group(PS[br], [(W[:, slot, k2 * 256 + oc * 128:k2 * 256 + (oc + 1) * 128], XBb[:, k2, :])
                                        for k2 in range(2)],
                               reads=[R("W", slot)] + rXBb, writes=[R("PS", br)])
                    B.mm_group(PS[bi], [(W[:, slot, 512 + k2 * 256 + oc * 128:512 + k2 * 256 + (oc + 1) * 128],
                                         XBb[:, k2, :]) for k2 in range(2)],
                               reads=[R("W", slot)] + rXBb, writes=[R("PS", bi)])
                    B.op("act", lambda e: e.activation(
                        out=FB[s][:, oc, :], in_=PS[br], func=AF.Sigmoid, bias=PV[:, BA + c:BA + c + 1], scale=1.0),
                        reads=[R("PS", br), R("PV")], writes=[R("FB", s, oc)], tab="sig")
                    B.op("act", lambda e: e.activation(
                        out=FB[s][:, 4 + oc, :], in_=PS[bi], func=AF.Sigmoid, bias=PV[:, BI + c:BI + c + 1], scale=1.0),
                        reads=[R("PS", bi), R("PV")], writes=[R("FB", s, 4 + oc)], tab="sig")
                rel(pi)
                yield 2.0
                Rr = [FB[s][:, oc, :] for oc in range(2)]
                Mm = [FB[s][:, 2 + oc, :] for oc in range(2)]
                Ii = [FB[s][:, 4 + oc, :] for oc in range(2)]
                rR = [[R("FB", s, oc)] for oc in range(2)]
                rM = [[R("FB", s, 2 + oc)] for oc in range(2)]
                rI = [[R("FB", s, 4 + oc)] for oc in range(2)]
                cs = [2 * hd + oc for oc in range(2)]
                for oc in range(2):
                    B.op("act", lambda e, oc=oc: e.activation(out=Mm[oc], in_=Rr[oc], func=AF.Exp,
                                                              scale=CL[:, 16 + cs[oc]:16 + cs[oc] + 1]),
                         reads=rR[oc] + [R("CL")], writes=rM[oc], tab="exp")
                    B.op("act", lambda e, oc=oc: e.activation(out=Rr[oc], in_=Rr[oc], func=AF.Exp,
                                                              scale=CL[:, 8 + cs[oc]:8 + cs[oc] + 1]),
                         reads=rR[oc] + [R("CL")], writes=rR[oc], tab="exp")
                for oc in range(2):
                    B.op("act", lambda e, oc=oc: e.activation(out=Mm[oc], in_=Mm[oc], func=AF.Sqrt,
                                                              bias=ONEC[:, 0:1], scale=-1.0),
                         reads=rM[oc], writes=rM[oc], tab="sqrt")
                for oc in range(2):
                    B.op("dve", lambda e, oc=oc: e.tensor_tensor(out=Ii[oc], in0=Ii[oc], in1=Mm[oc], op=ALU.mult),
                         reads=rI[oc] + rM[oc], writes=rI[oc], f=2.0)
                for oc in range(2):
                    B.op("dve", lambda e, oc=oc: e.tensor_tensor(out=Ii[oc], in0=Ii[oc], in1=XBv[:, oc, :], op=ALU.mult),
                         reads=rI[oc] + [R("XBc", s, oc)] + bigu(s, 4608 + oc * 512, 5120 + oc * 512),
                         writes=rI[oc], f=2.0)
                yield 4.0
                for oc in range(2):
                    B.op("dve", lambda e, oc=oc: e.tensor_tensor_scan(
                        out=Mm[oc], data0=Rr[oc], data1=Ii[oc], initial=HST[:, cs[oc]:cs[oc] + 1],
                        op0=ALU.mult, op1=ALU.add),
                        reads=rR[oc] + rI[oc] + [R("HST", cs[oc])], writes=rM[oc], f=2.0)
                for oc in range(2):
                    B.op("dve", lambda e, oc=oc: e.tensor_copy(out=HST[:, cs[oc]:cs[oc] + 1], in_=Mm[oc][:, T - 1:T]),
                         reads=rM[oc], writes=[R("HST", cs[oc])], n=1)
                for oc in range(2):
                    B.op("dve", lambda e, oc=oc: e.tensor_tensor(out=HY[s][:, cs[oc], :], in0=Mm[oc], in1=Yv[:, oc, :],
                                                                 op=ALU.mult),
                         reads=rM[oc] + rY, writes=[R("HY", s, cs[oc])], f=2.0)
                yield 4.0
            lru_done[k] = True
            yield from out_proj(s, k, "lout", BO, PV, NM + 8, presq, final)

        SCALE = 0.125
        NB_ = T // 128

        def attn(s, k, presq, final=None):
            BIGb = BIG[s][:, :].bitcast(BF16)
            QZ = BIGb[:, 0:8192].rearrange("p (c h t) -> p c h t", c=8, h=2)
            KT = BIGb[:, 8192:8192 + 128 + T]
            VP = BIGb[:, 8832:8832 + (NB_ + 1) * 256].rearrange("p (b h m) -> p b h m", b=NB_ + 1, h=2)
            rQZ = bigu(s, 0, 4096)
            rKT = bigu(s, 4096, 4416)
            rVP = bigu(s, 4416, 4416 + (NB_ + 1) * 128)
            EEs = [FB[s][:, kk, :].bitcast(BF16).rearrange("p (h q) -> p h q", h=4) for kk in range(3)]
            PTs = [FB[s][:, 3 + kk, :].bitcast(BF16).rearrange("p (a q) -> p a q", a=8) for kk in range(3)]
            rEE = [R("FB", s, kk) for kk in range(3)]
            rPT = [R("FB", s, 3 + kk) for kk in range(3)]
            norm_in(s, NM + 16)
            yield 10.0
            if final is not None:
                early_io(s, final)
            B.op("dve", lambda e: e.memset(QZ[64:128, :, 0, :], 0.0), writes=rQZ, n=4096)
            B.op("dve", lambda e: e.memset(QZ[0:64, :, 1, :], 0.0), writes=rQZ, n=4096)
            B.op("dve", lambda e: e.memset(VP[:, 1:NB_ + 1, 0, 64:128], 0.0), writes=rVP, n=256)
            B.op("dve", lambda e: e.memset(VP[:, 1:NB_ + 1, 1, 0:64], 0.0), writes=rVP, n=256)
            yield 9.0
            while k > 0 and not akv_done[k - 1]:
                yield None
            B.op("dve", lambda e: e.tensor_copy(out=KT[:, 0:128], in_=KC[:, :]), reads=[R("KC")], writes=rKT)
            B.op("dve", lambda e: e.tensor_copy(out=VP[:, 0, :, :], in_=VC[:, :, :]), reads=[R("VC")], writes=rVP)
            pi = acq(("akv", k))
            slot = ws(pi)
            b = nb(s)
            B.mm_group(PS[b], [(W[:, slot, kc * 128:(kc + 1) * 128], XN[s][:, kc, :]) for kc in range(NCH)],
                       reads=[R("W", slot)] + [R("XN", s, kc) for kc in range(NCH)], writes=[R("PS", b)])
            B.op("act", lambda e: e.activation(out=KT[:, 128:128 + T], in_=PS[b], func=AF.Identity,
                                               bias=PV[:, BK:BK + 1], scale=1.0),
                 reads=[R("PS", b), R("PV")], writes=rKT)
            b = nb(s)
            for blk in range(NB_):
                B.mm_group(PS[b][:, blk * 128:(blk + 1) * 128],
                           [(XN[s][:, kc, blk * 128:(blk + 1) * 128], W[:, slot, 1024 + kc * 128:1024 + (kc + 1) * 128])
                            for kc in range(NCH)],
                           reads=[R("W", slot)] + [R("XN", s, kc) for kc in range(NCH)], writes=[R("PS", b)])
            rel(pi)
            PSv = PS[b].rearrange("p (b m) -> p b m", b=NB_)
            B.op("dve", lambda e: e.tensor_tensor(
                out=VP[:, 1:NB_ + 1, 0, 0:64], in0=PSv[:, :, 0:64],
                in1=BC[:, 0:64].unsqueeze(1).broadcast_to([128, NB_, 64]), op=ALU.add),
                reads=[R("PS", b), R("BC")], writes=rVP)
            B.op("dve", lambda e: e.tensor_tensor(
                out=VP[:, 1:NB_ + 1, 1, 64:128], in0=PSv[:, :, 64:128],
                in1=BC[:, 64:128].unsqueeze(1).broadcast_to([128, NB_, 64]), op=ALU.add),
                reads=[R("PS", b), R("BC")], writes=rVP)
            B.op("dve", lambda e: e.tensor_copy(out=KC[:, :], in_=KT[:, T:T + 128]), reads=rKT, writes=[R("KC")])
            B.op("dve", lambda e: e.tensor_copy(out=VC[:, :, :], in_=VP[:, NB_, :, :]), reads=rVP, writes=[R("VC")])
            akv_done[k] = True
            yield 4.0
            for pq in range(4):
                pi = acq(("aq", k, pq))
                slot = ws(pi)
                for c2 in range(2):
                    c = 2 * pq + c2
                    b = nb(s)
                    B.mm_group(PS[b],
                               [(W[:, slot, (c2 * 8 + kc) * 128:(c2 * 8 + kc + 1) * 128], XN[s][:, kc, :])
                                for kc in range(NCH)],
                               reads=[R("W", slot)] + [R("XN", s, kc) for kc in range(NCH)], writes=[R("PS", b)])
                    B.op("act", lambda e: e.activation(
                        out=QZ[0:64, c, 0, :], in_=PS[b][0:64, :], func=AF.Identity,
                        bias=PV[0:64, BQ + c:BQ + c + 1], scale=1.0),
                        reads=[R("PS", b), R("PV")], writes=bigu(s, c * 512, (c + 1) * 512))
                    B.op("act", lambda e: e.activation(
                        out=QZ[64:128, c, 1, :], in_=PS[b][64:128, :], func=AF.Identity,
                        bias=PV[64:128, BQ + c:BQ + c + 1], scale=1.0),
                        reads=[R("PS", b), R("PV")], writes=bigu(s, c * 512, (c + 1) * 512))
                rel(pi)
                yield 3.8

            steps = [(n, g) for n in range(NB_) for g in range(4)]
            NSTEP = len(steps)

            def phaseA(i):
                n, g = steps[i]
                kk = i % 3
                b = nb2(s)
                mk = MASK0 if (k == 0 and n == 0) else MASK
                for hp in range(2):
                    c = 2 * g + hp
                    for hf in range(2):
                        B.mm_group(PSALL[:, (b + hp) * 512 + hf * 256:(b + hp) * 512 + (hf + 1) * 256],
                                   [(QZ[:, c, hf, n * 128:(n + 1) * 128], KT[:, n * 128:n * 128 + 256]),
                                    (IDENT[:, :], mk[:, :])],
                                   reads=bigu(s, c * 512, (c + 1) * 512) + rKT + [R("IDENT"), R("MASK"), R("MASK0")],
                                   writes=[R("PS", b + hp)])
                for hp in range(2):
                    bank = b + hp
                    mxc = 0 if hp == 0 else 14
                    ngc = 1 if hp == 0 else 15
                    B.op("dve", lambda e, bank=bank, mxc=mxc: e.tensor_reduce(
                        out=ST[s][:, kk, mxc:mxc + 1], in_=PS[bank], axis=AX.X, op=ALU.max),
                        reads=[R("PS", bank)], writes=[R("ST", s, kk, 0, hp)], n=512)
                    B.op("dve", lambda e, mxc=mxc, ngc=ngc: e.tensor_scalar(
                        out=ST[s][:, kk, ngc:ngc + 1], in0=ST[s][:, kk, mxc:mxc + 1], scalar1=-SCALE,
                        scalar2=NSMAX[:, 0:1], op0=ALU.mult, op1=ALU.min),
                        reads=[R("ST", s, kk, 0, hp), R("NSMAX")], writes=[R("ST", s, kk, 1, hp)], n=4)
                    for hf in range(2):
                        hh = 2 * hp + hf
                        B.op("act", lambda e, bank=bank, hf=hf, hh=hh, ngc=ngc: e.activation(
                            out=EEs[kk][:, hh, :], in_=PSALL[:, bank * 512 + hf * 256:bank * 512 + (hf + 1) * 256],
                            func=AF.Exp, bias=ST[s][:, kk, ngc:ngc + 1], scale=SCALE,
                            accum_out=ST[s][:, kk, 2 + hh:3 + hh]),
                            reads=[R("PS", bank), R("ST", s, kk, 1, hp)], writes=[rEE[kk], R("ST", s, kk, 2)],
                            n=400, tab="exp")
                    B.op("act", lambda e, hp=hp, ngc=ngc: e.activation(
                        out=ST[s][:, kk, 6 + 2 * hp:8 + 2 * hp], in_=BC[:, 128 + 4 * g + 2 * hp:128 + 4 * g + 2 * hp + 2],
                        func=AF.Exp, bias=ST[s][:, kk, ngc:ngc + 1], scale=1.0),
                        reads=[R("ST", s, kk, 1, hp), R("BC")], writes=[R("ST", s, kk, 3)], n=4, tab="exp")

            def phaseB(i):
                n, g = steps[i]
                kk = i % 3
                B.op("dve", lambda e: e.tensor_tensor(out=ST[s][:, kk, 10:14], in0=ST[s][:, kk, 2:6],
                                                      in1=ST[s][:, kk, 6:10], op=ALU.add),
                     reads=[R("ST", s, kk, 2), R("ST", s, kk, 3)], writes=[R("ST", s, kk, 4)], n=4)
                B.op("dve", lambda e: e.reciprocal(out=ST[s][:, kk, 10:14], in_=ST[s][:, kk, 10:14]),
                     reads=[R("ST", s, kk, 4)], writes=[R("ST", s, kk, 4)], n=32)
                for hh in range(4):
                    B.op("dve", lambda e, hh=hh: e.tensor_scalar(
                        out=EEs[kk][:, hh, :], in0=EEs[kk][:, hh, :], scalar1=ST[s][:, kk, 10 + hh:11 + hh],
                        scalar2=None, op0=ALU.mult),
                        reads=[rEE[kk], R("ST", s, kk, 4)], writes=[rEE[kk]], n=256)

            def phaseB2(i):
                n, g = steps[i]
                kk = i % 3
                bt = nb(s)
                PTb = PS[bt].bitcast(BF16)
                def emit_tr():
                    ins = None
                    for hh in range(4):
                        for k2 in range(2):
                            a_ = hh * 2 + k2
                            ins = nc.tensor.transpose(PTb[:, a_ * 128:(a_ + 1) * 128],
                                                      EEs[kk][:, hh, k2 * 128:(k2 + 1) * 128], IDENT[:, :])
                    return ins
                B.pe_manual(emit_tr, [rEE[kk], R("IDENT")], [R("PS", bt)], 8 * 0.08)
                B.op("act", lambda e: e.activation(out=PTs[kk], in_=PTb.rearrange("p (a q) -> p a q", a=8),
                                                   func=AF.Copy),
                     reads=[R("PS", bt)], writes=[rPT[kk]], n=1024)

            def phaseC(i):
                n, g = steps[i]
                kk = i % 3
                bo = nb(s)
                for ci in range(2):
                    B.mm_group(PS[bo][:, ci * 128:(ci + 1) * 128],
                               [(VP[:, n + k2, hf, :], PTs[kk][:, (ci * 2 + hf) * 2 + k2, :])
                                for hf in range(2) for k2 in range(2)],
                               reads=rVP + [rPT[kk]], writes=[R("PS", bo)])
                B.op("act", lambda e: e.activation(
                    out=HY[s][:, 2 * g:2 * g + 2, n * 128:(n + 1) * 128],
                    in_=PS[bo][:, 0:256].rearrange("p (c q) -> p c q", c=2), func=AF.Copy),
                    reads=[R("PS", bo)], writes=[R("HY", s, 2 * g), R("HY", s, 2 * g + 1)], n=256)

            for i in range(NSTEP + 4):
                if i < NSTEP:
                    phaseA(i)
                if 0 <= i - 1 < NSTEP:
                    phaseB(i - 1)
                if 0 <= i - 2 < NSTEP:
                    phaseB2(i - 2)
                if 0 <= i - 4 < NSTEP:
                    phaseC(i - 4)
                if i % ay == ay - 1:
                    yield 3.6
            yield from out_proj(s, k, "ao", AO, PV, NM + 24, presq, final)

        def chain(s):
            for k in range(s, ntiles, NS_):
                if k < NS_:
                    for i in range(NCH):
                        B.dma("sp", xsem[s][i], X[s][:, i, :], xT3[:, i, k * T:(k + 1) * T],
                              writes=[R("X", s, i)], nbytes=1 << 18)
                yield 1.0
                sq_ready[s] = False
                for si, st in enumerate(stages):
                    last = si + 1 == len(stages)
                    presq = not last
                    final = k if last else None
                    if st.startswith("ffn"):
                        yield from ffn(s, k, int(st[3]), presq, final)
                    elif st == "lru":
                        yield from lru(s, k, presq, final)
                    elif st == "attn":
                        yield from attn(s, k, presq, final)
                yield 1.0

        issue()
        gens = [chain(s) for s in range(NS_)]
        start_off = [0.0, offset]
        alive = [True, True]
        while any(alive):
            order = sorted([s for s in range(NS_) if alive[s]], key=lambda s: max(start_off[s], B.stime[s]))
            progressed = False
            for s in order:
                B.cur = s
                try:
                    r = next(gens[s])
                except StopIteration:
                    alive[s] = False
                    progressed = True
                    break
                if r is None:
                    continue
                progressed = True
                break
            assert progressed, "scheduler deadlock"
        for s in range(NS_):
            for i in range(NCH):
                B._wait("sp", osem[s][i], osem[s][i].v)
                B._wait("sp", asem[s][i], asem[s][i].v)
    return nc


def _tile_kn(w):
    k, n = w.shape
    return w.reshape(k // 128, 128, n).transpose(1, 0, 2)


def _cols(v):
    return np.ascontiguousarray(v.reshape(-1, 128).T)


def prepare_shared(inp):
    f = np.float32
    g = {k: np.asarray(v, dtype=f) for k, v in inp.items() if k != "x"}
    wgu = np.empty((4, NFF, 128, 2048), f)
    wdn = np.empty((4, 16, 128, 1408), f)
    for l in range(2):
        for w in range(2):
            fi = l * 2 + w
            wg = _tile_kn(g["ffn_w_gate"][l, w]).reshape(128, 8, NFF, 128).transpose(2, 0, 1, 3)
            wu = _tile_kn(g["ffn_w_up"][l, w]).reshape(128, 8, NFF, 128).transpose(2, 0, 1, 3)
            wgu[fi, :, :, 0:1024] = wg.reshape(NFF, 128, 1024)
            wgu[fi, :, :, 1024:2048] = wu.reshape(NFF, 128, 1024)
            wd = _tile_kn(g["ffn_w_down"][l, w])
            wd = wd.reshape(128, 2, 11, 8, 128).transpose(3, 1, 0, 2, 4)
            wdn[fi] = wd.reshape(16, 128, 1408)
    wmix = np.zeros((25, 128, 2048), f)
    win = _tile_kn(g["lru_w_in"][0])
    for c in range(8):
        wmix[WM_LIN + c, :, 0:1024] = win[:, :, c * 128:(c + 1) * 128].reshape(128, 1024)
        wmix[WM_LIN + c, :, 1024:2048] = win[:, :, 1024 + c * 128:1024 + (c + 1) * 128].reshape(128, 1024)
    for hd in range(4):
        wmix[WM_LG + hd, :, 0:512] = _tile_kn(g["lru_w_a"][0, hd]).reshape(128, 512)
        wmix[WM_LG + hd, :, 512:1024] = _tile_kn(g["lru_w_i"][0, hd]).reshape(128, 512)
    wout = _tile_kn(g["lru_w_out"][0])
    for ip in range(4):
        for i2 in range(2):
            i = 2 * ip + i2
            wmix[WM_LOUT + ip, :, i2 * 1024:(i2 + 1) * 1024] = wout[:, :, i * 128:(i + 1) * 128].reshape(128, 1024)
    wqkv = g["attn_w_qkv"][0]
    qcols = np.concatenate([np.r_[c * 64:(c + 1) * 64, (8 + c) * 64:(9 + c) * 64] for c in range(8)])
    wq = _tile_kn(wqkv[:, qcols])
    for pq in range(4):
        for c2 in range(2):
            c = 2 * pq + c2
            wmix[WM_AQ + pq, :, c2 * 1024:(c2 + 1) * 1024] = wq[:, :, c * 128:(c + 1) * 128].reshape(128, 1024)
    wmix[WM_AKV, :, 0:1024] = _tile_kn(wqkv[:, 1024:1152]).reshape(128, 1024)
    wmix[WM_AKV, :, 1024:2048] = _tile_kn(wqkv[:, 1152:1280]).reshape(128, 1024)
    wo = _tile_kn(g["attn_w_o"][0][qcols, :])
    for ip in range(4):
        for i2 in range(2):
            i = 2 * ip + i2
            wmix[WM_AO + ip, :, i2 * 1024:(i2 + 1) * 1024] = wo[:, :, i * 128:(i + 1) * 128].reshape(128, 1024)
    pv = np.zeros((128, NPV), f)
    pv[:, NF:NF + 64] = _cols(g["norm_ffn"].reshape(-1))
    pv[:, NM:NM + 32] = _cols(g["norm_mix"].reshape(-1))
    pv[:, BIN:BIN + 16] = _cols(g["lru_b_in"][0])
    pv[:, CW:CW + 32] = _cols(g["lru_conv_w"][0].reshape(-1))
    pv[:, CB:CB + 8] = _cols(g["lru_conv_b"][0])
    pv[:, BA:BA + 8] = _cols(g["lru_b_a"][0])
    pv[:, BI:BI + 8] = _cols(g["lru_b_i"][0])
    pv[:, LAM:LAM + 8] = _cols(g["lru_lambda"][0])
    pv[:, BO:BO + 8] = _cols(g["lru_b_out"][0])
    bqkv = g["attn_b_qkv"][0]
    pv[:, BQ:BQ + 8] = _cols(bqkv[qcols])
    pv[:, BK:BK + 1] = _cols(bqkv[1024:1152])
    pv[:, AO:AO + 8] = _cols(g["attn_b_o"][0])
    bc = np.zeros((128, 144), f)
    bc[:, 0:128] = np.broadcast_to(bqkv[1152:1280], (128, 128))
    horder = np.array([hf * 8 + c for c in range(8) for hf in range(2)])
    bc[:, 128:144] = np.broadcast_to(g["attn_sinks"][0][horder], (128, 16))
    return dict(wgu=wgu.reshape(4 * NFF, 128, 2048), wdn=wdn.reshape(64, 128, 1408), wmix=wmix, pvec=pv, bcast=bc)


_PROG = {}


def kernel(**inputs):
    x = np.asarray(inputs["x"], dtype=np.float32)
    shared = prepare_shared(inputs)
    if "full" not in _PROG:
        _PROG["full"] = build_program()
    nc = _PROG["full"]
    in_maps = []
    for b in range(NCORES):
        m = dict(shared)
        m["xT"] = np.ascontiguousarray(x[b].T)
        in_maps.append(m)
    res = run_bass_kernel_spmd(nc, in_maps, core_ids=list(range(NCORES)))
    out = np.stack([np.ascontiguousarray(r["outT"].T) for r in res.results], axis=0)
    return out.astype(np.float32)
```

```python
import numpy as np
from contextlib import ExitStack
import concourse.bass as bass
import concourse.mybir as mybir
from concourse.bass_utils import run_bass_kernel_spmd

F32 = mybir.dt.float32
BF16 = mybir.dt.bfloat16
AF = mybir.ActivationFunctionType
ALU = mybir.AluOpType
AX = mybir.AxisListType

D = 1024
SEQ = 4096
DFF = 2816
NCH = 8
NFF = 22
T = 512
NT = SEQ // T
NBLK = T // 128
EPS = 1e-6
NCORES = 8
NWS = 8
WCOLS = 2048
ALL_STAGES = ("ffn0", "lru", "ffn1", "ffn2", "attn", "ffn3")

NF, NM, BIN, CW, CB, BA, BI, LAM, BO, BQ, BK, AO, NPV = 0, 64, 96, 112, 144, 152, 160, 168, 176, 184, 192, 200, 208
WM_LIN, WM_LG, WM_LOUT, WM_AQ, WM_AKV, WM_AO = 0, 8, 12, 16, 20, 21


class Sem:
    def __init__(self, nc, es, name):
        self.h = es.enter_context(nc.semaphore(name))
        self.v = 0


class Res:
    __slots__ = ("lw", "rd")

    def __init__(self):
        self.lw = None
        self.rd = {}


class Builder:
    BASE = dict(pe=0.01, act=0.25, dve=0.12, pool=0.5, sp=0.1)
    RATE = dict(pe=2370.0, act=1400.0, dve=960.0, pool=300.0, sp=1e9)
    HOP = 0.3

    def __init__(self, nc, es):
        self.nc = nc
        self.es = es
        self.eng = dict(pe=nc.tensor, act=nc.scalar, dve=nc.vector, pool=nc.gpsimd, sp=nc.sync)
        self.esem = {k: Sem(nc, es, "s_" + k) for k in self.eng}
        self.waited = {k: {} for k in self.eng}
        self.res = {}
        self.etime = {k: 0.0 for k in self.eng}
        self.ttime = {}
        self.dma_free = 0.0
        self.cur = 0
        self.stime = [0.0, 0.0]
        self.acttab = None
        self.ntab = 0

    def R(self, *key):
        r = self.res.get(key)
        if r is None:
            r = self.res[key] = Res()
        return r

    def _wait(self, en, sem, val):
        w = self.waited[en]
        if w.get(sem, 0) >= val:
            return
        self.eng[en].wait_ge(sem.h, val)
        w[sem] = val

    def _deps(self, en, reads, writes):
        need = {}
        for r in reads:
            if r.lw is not None:
                s, v = r.lw
                if need.get(s, 0) < v:
                    need[s] = v
        for r in writes:
            if r.lw is not None:
                s, v = r.lw
                if need.get(s, 0) < v:
                    need[s] = v
            for s, v in r.rd.items():
                if need.get(s, 0) < v:
                    need[s] = v
        ready = 0.0
        for s, v in need.items():
            self._wait(en, s, v)
            t = self.ttime.get((s, v), 0.0)
            if t > ready:
                ready = t
        return ready

    def _reg(self, tok, reads, writes):
        s, v = tok
        for r in writes:
            r.lw = tok
            r.rd = {}
        for r in reads:
            if r.rd.get(s, 0) < v:
                r.rd[s] = v

    def _time(self, en, ready, dur, tok):
        start = max(self.etime[en], ready + self.HOP)
        end = start + dur
        self.etime[en] = end
        self.ttime[tok] = end
        if end > self.stime[self.cur]:
            self.stime[self.cur] = end

    def op(self, en, fn, reads=(), writes=(), n=512, f=1.0, tab=None):
        ready = self._deps(en, reads, writes)
        if tab is not None and tab != self.acttab:
            self.acttab = tab
            self.etime[en] = max(self.etime[en], ready) + 1.28
            self.ntab += 1
        ins = fn(self.eng[en])
        sem = self.esem[en]
        sem.v += 1
        ins.then_inc(sem.h, 1)
        tok = (sem, sem.v)
        self._reg(tok, reads, writes)
        feff = 1.0 + 0.35 * (f - 1.0) if f <= 2.0 else f
        self._time(en, ready, self.BASE[en] + feff * n / self.RATE[en], tok)
        return tok

    def dma(self, en, sem, out, in_, reads=(), writes=(), nbytes=1 << 20, accum=False):
        ready = self._deps(en, reads, writes)
        if accum:
            ins = self.eng[en].dma_start(out=out, in_=in_, accum_op=ALU.add)
        else:
            ins = self.eng[en].dma_start(out=out, in_=in_)
        sem.v += 16
        ins.then_inc(sem.h, 16)
        tok = (sem, sem.v)
        self._reg(tok, reads, writes)
        if en == "pool":
            self.ttime[tok] = 0.0
            return tok
        issue_end = max(self.etime[en], ready + self.HOP) + 0.1
        self.etime[en] = issue_end
        xs = max(issue_end, self.dma_free)
        self.dma_free = xs + nbytes / 300e3
        self.ttime[tok] = self.dma_free + 2.0
        return tok

    def mm_group(self, out_ap, pairs, reads, writes):
        ready = self._deps("pe", reads, writes)
        n = len(pairs)
        ins = None
        dur = 0.0
        for i, (l, r) in enumerate(pairs):
            ins = self.nc.tensor.matmul(out_ap, lhsT=l, rhs=r, start=(i == 0), stop=(i == n - 1))
            dur += max(r.free_size(), 96) / self.RATE["pe"] + 0.005
        sem = self.esem["pe"]
        sem.v += 1
        ins.then_inc(sem.h, 1)
        tok = (sem, sem.v)
        self._reg(tok, reads, writes)
        self._time("pe", ready, dur, tok)
        return tok

    def mm_group_seq(self, out_ap, pairs, per_pair_reads, common_reads, writes):
        ready0 = self._deps("pe", common_reads, writes)
        n = len(pairs)
        ins = None
        allreads = list(common_reads)
        for i, (l, r) in enumerate(pairs):
            ri = self._deps("pe", per_pair_reads[i], ())
            allreads += list(per_pair_reads[i])
            ins = self.nc.tensor.matmul(out_ap, lhsT=l, rhs=r, start=(i == 0), stop=(i == n - 1))
            start = max(self.etime["pe"], max(ready0, ri) + self.HOP)
            self.etime["pe"] = start + max(r.free_size(), 96) / self.RATE["pe"] + 0.005
        sem = self.esem["pe"]
        sem.v += 1
        ins.then_inc(sem.h, 1)
        tok = (sem, sem.v)
        self._reg(tok, allreads, writes)
        end = self.etime["pe"]
        self.ttime[tok] = end
        if end > self.stime[self.cur]:
            self.stime[self.cur] = end
        return tok

    def pe_manual(self, emit, reads, writes, dur):
        ready = self._deps("pe", reads, writes)
        ins = emit()
        sem = self.esem["pe"]
        sem.v += 1
        ins.then_inc(sem.h, 1)
        tok = (sem, sem.v)
        self._reg(tok, reads, writes)
        self._time("pe", ready, dur, tok)
        return tok


def build_program(stages=ALL_STAGES, ntiles=NT, plan_keys=None, offset=255.0, gy=2, dy=1, ay=1):
    if plan_keys is None:
        rec = []
        build_program(stages, ntiles, plan_keys=rec, offset=offset, gy=gy, dy=dy, ay=ay)
        plan_keys = tuple(rec)
        recording = False
    else:
        recording = isinstance(plan_keys, list)
    nc = bass.Bass("TRN2", target_bir_lowering=False)
    xT = nc.dram_tensor("xT", [D, SEQ], F32, kind="ExternalInput").ap()
    wgu = nc.dram_tensor("wgu", [4 * NFF, 128, 2048], F32, kind="ExternalInput").ap()
    wdn = nc.dram_tensor("wdn", [4 * 16, 128, 1408], F32, kind="ExternalInput").ap()
    wmix = nc.dram_tensor("wmix", [25, 128, 2048], F32, kind="ExternalInput").ap()
    pvec = nc.dram_tensor("pvec", [128, NPV], F32, kind="ExternalInput").ap()
    bcast = nc.dram_tensor("bcast", [128, 144], F32, kind="ExternalInput").ap()
    outT = nc.dram_tensor("outT", [D, SEQ], F32, kind="ExternalOutput").ap()
    xT3 = xT.rearrange("(kc p) t -> p kc t", p=128)
    outT3 = outT.rearrange("(kc p) t -> p kc t", p=128)

    with ExitStack() as es:
        def sb(name, shape, dt):
            return es.enter_context(nc.sbuf_tensor(name, shape, dt))

        NS_ = 2
        X = [sb(f"X{s}", [128, NCH, T], F32) for s in range(NS_)]
        XN = [sb(f"XN{s}", [128, NCH, T], BF16) for s in range(NS_)]
        BIG = [sb(f"BIG{s}", [128, 5632], F32) for s in range(NS_)]
        HY = [sb(f"HY{s}", [128, NCH, T], BF16) for s in range(NS_)]
        FB = [sb(f"FB{s}", [128, NCH, T], F32) for s in range(NS_)]
        SQ = [sb(f"SQ{s}", [128, NCH, T], BF16) for s in range(NS_)]
        RT = [sb(f"RT{s}", [128, T], F32) for s in range(NS_)]
        RS = [sb(f"RS{s}", [128, T], F32) for s in range(NS_)]
        SG = [sb(f"SG{s}", [128, 2, T], BF16) for s in range(NS_)]
        ST = [sb(f"ST{s}", [128, 3, 16], F32) for s in range(NS_)]
        W = sb("W", [128, NWS, WCOLS], BF16)
        PV = sb("PV", [128, NPV], F32)
        BC = sb("BC", [128, 144], F32)
        CG = sb("CG", [128, 64], F32)
        CL = sb("CL", [128, 24], F32)
        NSINK = sb("NSINK", [128, 16], F32)
        NSMAX = sb("NSMAX", [128, 1], F32)
        ONES = sb("ONES", [128, 128], BF16)
        IDENT = sb("IDENT", [128, 128], BF16)
        MASK = sb("MASK", [128, 256], BF16)
        MASK0 = sb("MASK0", [128, 256], BF16)
        MTMP = sb("MTMP", [128, 256], F32)
        HST = sb("HST", [128, 8], F32)
        CARRY = sb("CARRY", [128, 8, 4], F32)
        KC = sb("KC", [128, 128], BF16)
        VC = sb("VC", [128, 2, 128], BF16)
        EPSC = sb("EPSC", [128, 1], F32)
        ONEC = sb("ONEC", [128, 1], F32)
        PSALL = es.enter_context(nc.psum_tensor("psall", [128, 8 * 512], F32))
        PS = [PSALL[:, i * 512:(i + 1) * 512] for i in range(8)]

        B = Builder(nc, es)
        R = B.R
        wsem = [Sem(nc, es, f"w{i}") for i in range(NWS)]
        xsem = [[Sem(nc, es, f"xl{s}_{i}") for i in range(NCH)] for s in range(NS_)]
        osem = [[Sem(nc, es, f"xo{s}_{i}") for i in range(NCH)] for s in range(NS_)]
        asem = [[Sem(nc, es, f"xa{s}_{i}") for i in range(NCH)] for s in range(NS_)]
        csem = Sem(nc, es, "cst")
        bank_i = [0, 0]

        def nb(s):
            b = 4 * s + bank_i[s] % 4
            bank_i[s] += 1
            return b

        def nb2(s):
            if bank_i[s] % 2:
                bank_i[s] += 1
            b = 4 * s + bank_i[s] % 4
            bank_i[s] += 2
            return b

        def piece_src(key):
            kind = key[0]
            if kind == "gu":
                return wgu[key[2] * NFF + key[3]], 2048
            if kind == "d":
                return wdn[key[2] * 16 + key[3] * 2 + key[4]], 1408
            if kind == "lin":
                return wmix[WM_LIN + key[2]], 2048
            if kind == "lg":
                return wmix[WM_LG + key[2]], 1024
            if kind == "lout":
                return wmix[WM_LOUT + key[2]], 2048
            if kind == "aq":
                return wmix[WM_AQ + key[2]], 2048
            if kind == "akv":
                return wmix[WM_AKV], 2048
            if kind == "ao":
                return wmix[WM_AO + key[2]], 2048
            raise KeyError(key)

        wstate = dict(issued=0, acq=0)
        released = []

        def issue_one():
            i = wstate["issued"]
            slot = i % NWS
            src, ncols = piece_src(plan_keys[i])
            B.dma("pool", wsem[slot], W[:, slot, 0:ncols], src[:, 0:ncols], writes=[R("W", slot)],
                  nbytes=128 * ncols * 4)
            wstate["issued"] += 1

        def issue():
            if recording:
                return
            while wstate["issued"] < len(plan_keys):
                i = wstate["issued"]
                if i >= NWS and not (i - NWS < len(released) and released[i - NWS]):
                    break
                issue_one()

        def acq(key):
            i = wstate["acq"]
            wstate["acq"] += 1
            released.append(False)
            if recording:
                plan_keys.append(key)
                assert i < NWS or released[i - NWS], "too many weight pieces open"
                issue_one()
            else:
                assert plan_keys[i] == key, (plan_keys[i], key)
                assert i < wstate["issued"], "weight piece not prefetched (too many open)"
            return i

        def rel(i):
            released[i] = True
            issue()

        def ws(i):
            return i % NWS

        B.dma("sp", csem, PV[:, :], pvec[:, :], writes=[R("PV")])
        B.dma("sp", csem, BC[:, :], bcast[:, :], writes=[R("BC")])
        for en in ("act", "dve", "pool"):
            B._wait(en, csem, csem.v)
        B.op("dve", lambda e: e.memset(ONES[:, :], 1.0), writes=[R("ONES")])
        B.op("dve", lambda e: e.memset(HST[:, :], 0.0), writes=[R("HST", c) for c in range(8)])
        B.op("dve", lambda e: e.memset(CARRY[:, :, :], 0.0), writes=[R("CARRY")])
        B.op("dve", lambda e: e.memset(KC[:, :], 0.0), writes=[R("KC")])
        B.op("dve", lambda e: e.memset(VC[:, :, :], 0.0), writes=[R("VC")])
        B.op("dve", lambda e: e.memset(MTMP[:, :], 0.0), writes=[R("MTMP")])
        B.op("dve", lambda e: e.memset(EPSC[:, :], EPS), writes=[R("EPSC")])
        B.op("dve", lambda e: e.memset(ONEC[:, :], 1.0), writes=[R("ONEC")])
        B.op("dve", lambda e: e.tensor_scalar(out=CG[:, :], in0=PV[:, NF:NF + 64], scalar1=0.5, scalar2=None,
                                              op0=ALU.mult), reads=[R("PV")], writes=[R("CG")])
        B.op("dve", lambda e: e.tensor_scalar(out=NSINK[:, :], in0=BC[:, 128:144], scalar1=-1.0, scalar2=None,
                                              op0=ALU.mult), reads=[R("BC")], writes=[R("NSINK")])
        B.op("dve", lambda e: e.tensor_reduce(out=NSMAX[:, :], in_=NSINK[:, :], axis=AX.X, op=ALU.min),
             reads=[R("NSINK")], writes=[R("NSMAX")])
        B.op("act", lambda e: e.activation(out=CL[:, 0:8], in_=PV[:, LAM:LAM + 8], func=AF.Exp, scale=-1.0),
             reads=[R("PV")], writes=[R("CL")], tab="exp")
        B.op("act", lambda e: e.activation(out=CL[:, 0:8], in_=CL[:, 0:8], func=AF.Ln, bias=1.0, scale=1.0),
             reads=[R("CL")], writes=[R("CL")], tab="exp")
        B.op("dve", lambda e: e.tensor_scalar(out=CL[:, 8:16], in0=CL[:, 0:8], scalar1=-8.0, scalar2=None,
                                              op0=ALU.mult), reads=[R("CL")], writes=[R("CL")])
        B.op("dve", lambda e: e.tensor_scalar(out=CL[:, 16:24], in0=CL[:, 0:8], scalar1=-16.0, scalar2=None,
                                              op0=ALU.mult), reads=[R("CL")], writes=[R("CL")])
        B.op("pool", lambda e: e.affine_select(out=IDENT[:, :], in_=ONES[:, :], pattern=[[-1, 128]],
                                               compare_op=ALU.is_equal, fill=0.0, base=0, channel_multiplier=1),
             reads=[R("ONES")], writes=[R("IDENT")])
        B.op("pool", lambda e: e.affine_select(out=MTMP[:, :], in_=MTMP[:, :], pattern=[[1, 256]],
                                               compare_op=ALU.is_ge, fill=-30000.0, base=-1, channel_multiplier=-1),
             reads=[R("MTMP")], writes=[R("MTMP")])
        B.op("pool", lambda e: e.affine_select(out=MASK[:, :], in_=MTMP[:, :], pattern=[[-1, 256]],
                                               compare_op=ALU.is_ge, fill=-30000.0, base=128, channel_multiplier=1),
             reads=[R("MTMP")], writes=[R("MASK")])
        B.op("pool", lambda e: e.affine_select(out=MASK0[:, :], in_=MASK[:, :], pattern=[[1, 256]],
                                               compare_op=ALU.is_ge, fill=-30000.0, base=-128, channel_multiplier=0),
             reads=[R("MASK")], writes=[R("MASK0")])
        for en in ("act", "pe"):
            B._wait(en, B.esem["dve"], B.esem["dve"].v)
        B._wait("pe", B.esem["pool"], B.esem["pool"].v)

        def norm_stats(s):
            b = nb(s)
            B.mm_group(PS[b], [(ONES[:, :], SQ[s][:, kc, :]) for kc in range(NCH)],
                       reads=[R("SQ", s, kc) for kc in range(NCH)] + [R("ONES")], writes=[R("PS", b)])
            B.op("act", lambda e: e.activation(out=RT[s][:, :], in_=PS[b], func=AF.Ln,
                                               bias=EPSC[:, 0:1], scale=1.0 / D),
                 reads=[R("PS", b)], writes=[R("RT", s)], tab="exp")
            B.op("act", lambda e: e.activation(out=PS[b], in_=RT[s][:, :], func=AF.Exp, scale=-0.5),
                 reads=[R("RT", s)], writes=[R("PS", b)], tab="exp")
            return b

        sq_ready = [False, False]

        def norm_in(s, gcol):
            if not sq_ready[s]:
                for kc in range(NCH):
                    B.op("act", lambda e, kc=kc: e.activation(out=SQ[s][:, kc, :], in_=X[s][:, kc, :], func=AF.Square),
                         reads=[R("X", s, kc)], writes=[R("SQ", s, kc)])
            sq_ready[s] = False
            b = norm_stats(s)
            for kc in range(NCH):
                B.op("dve", lambda e, kc=kc: e.scalar_tensor_tensor(
                    out=XN[s][:, kc, :], in0=X[s][:, kc, :], scalar=PV[:, gcol + kc:gcol + kc + 1],
                    in1=PS[b], op0=ALU.mult, op1=ALU.mult),
                    reads=[R("X", s, kc), R("PS", b), R("PV")], writes=[R("XN", s, kc)], f=1.0)

        def norm_out(s, gt, gcol, presq, final=None):
            b = norm_stats(s)

            def mult(i):
                B.op("dve", lambda e: e.tensor_tensor(out=FB[s][:, i, :], in0=FB[s][:, i, :],
                                                      in1=PS[b], op=ALU.mult),
                     reads=[R("FB", s, i), R("PS", b)], writes=[R("FB", s, i)], f=1.0)

            if final is None:
                mult(0)
            for i in range(NCH):
                if final is None and i + 1 < NCH:
                    mult(i + 1)
                if final is not None:
                    B.op("dve", lambda e, i=i: e.scalar_tensor_tensor(
                        out=FB[s][:, i, :], in0=FB[s][:, i, :], scalar=gt[:, gcol + i:gcol + i + 1],
                        in1=PS[b], op0=ALU.mult, op1=ALU.mult),
                        reads=[R("FB", s, i), R("PS", b), R("PV"), R("CG")], writes=[R("FB", s, i)], f=1.0)
                    B.dma("pool", asem[s][i], outT3[:, i, final * T:(final + 1) * T], FB[s][:, i, :],
                          reads=[R("FB", s, i)], writes=[R("OUT", s, i)], nbytes=1 << 18, accum=True)
                    continue
                B.op("dve", lambda e, i=i: e.scalar_tensor_tensor(
                    out=X[s][:, i, :], in0=FB[s][:, i, :], scalar=gt[:, gcol + i:gcol + i + 1],
                    in1=X[s][:, i, :], op0=ALU.mult, op1=ALU.add),
                    reads=[R("FB", s, i), R("X", s, i), R("PV"), R("CG")], writes=[R("X", s, i)], f=2.0)
                if presq:
                    B.op("act", lambda e, i=i: e.activation(out=SQ[s][:, i, :], in_=X[s][:, i, :], func=AF.Square),
                         reads=[R("X", s, i)], writes=[R("SQ", s, i)])
                if i % 2 == 1:
                    yield 1.0
            sq_ready[s] = presq

        xn_ready = [False, False]

        def early_io(s, k):
            for i in range(NCH):
                B.dma("sp", osem[s][i], outT3[:, i, k * T:(k + 1) * T], X[s][:, i, :],
                      reads=[R("X", s, i)], writes=[R("OUT", s, i)], nbytes=1 << 18)
            if k + NS_ < ntiles:
                for i in range(NCH):
                    B.dma("sp", xsem[s][i], X[s][:, i, :], xT3[:, i, (k + NS_) * T:(k + NS_ + 1) * T],
                          writes=[R("X", s, i)], nbytes=1 << 18)

        def early_stats(s):
            for kc in range(NCH):
                B.op("act", lambda e, kc=kc: e.activation(out=HY[s][:, kc, :], in_=X[s][:, kc, :], func=AF.Square),
                     reads=[R("X", s, kc)], writes=[R("HY", s, kc)])
            b = nb(s)
            B.mm_group(PS[b], [(ONES[:, :], HY[s][:, kc, :]) for kc in range(NCH)],
                       reads=[R("HY", s, kc) for kc in range(NCH)] + [R("ONES")], writes=[R("PS", b)])
            B.op("act", lambda e: e.activation(out=RT[s][:, :], in_=PS[b], func=AF.Ln,
                                               bias=EPSC[:, 0:1], scale=1.0 / D),
                 reads=[R("PS", b)], writes=[R("RT", s)], tab="exp")
            B.op("act", lambda e: e.activation(out=RT[s][:, :], in_=RT[s][:, :], func=AF.Exp, scale=-0.5),
                 reads=[R("RT", s)], writes=[R("RT", s)], tab="exp")

        def early_xn(s, gcol):
            for kc in range(NCH):
                B.op("dve", lambda e, kc=kc: e.scalar_tensor_tensor(
                    out=XN[s][:, kc, :], in0=X[s][:, kc, :], scalar=PV[:, gcol + kc:gcol + kc + 1],
                    in1=RT[s][:, :], op0=ALU.mult, op1=ALU.mult),
                    reads=[R("X", s, kc), R("RT", s), R("PV")], writes=[R("XN", s, kc)], f=2.0)
            xn_ready[s] = True

        def out_proj(s, k, kind, bcol, gt, gcol, presq, final=None):
            for ip in range(4):
                pi = acq((kind, k, ip))
                slot = ws(pi)
                for i2 in range(2):
                    i = 2 * ip + i2
                    bf = nb(s)
                    B.mm_group(PS[bf],
                               [(W[:, slot, (i2 * 8 + kc) * 128:(i2 * 8 + kc + 1) * 128], HY[s][:, kc, :])
                                for kc in range(NCH)],
                               reads=[R("W", slot)] + [R("HY", s, kc) for kc in range(NCH)],
                               writes=[R("PS", bf)])
                    B.op("act", lambda e: e.activation(
                        out=FB[s][:, i, :], in_=PS[bf], func=AF.Identity,
                        bias=PV[:, bcol + i:bcol + i + 1], scale=1.0),
                        reads=[R("PS", bf), R("PV")], writes=[R("FB", s, i)])
                    B.op("act", lambda e: e.activation(
                        out=SQ[s][:, i, :], in_=FB[s][:, i, :], func=AF.Square),
                        reads=[R("FB", s, i)], writes=[R("SQ", s, i)])
                rel(pi)
                yield 3.8
            yield from norm_out(s, gt, gcol, presq, final)

        sgi = [0, 0]

        def ffn(s, k, fi, presq, final=None):
            gin = NF + (fi * 2 + 0) * 8
            gout = (fi * 2 + 1) * 8
            A = BIG[s][:, :].bitcast(BF16).rearrange("p (j t) -> p j t", j=NFF)
            if xn_ready[s]:
                xn_ready[s] = False
            else:
                norm_in(s, gin)
            yield 10.0
            early = (final is not None and final + NS_ < ntiles and stages[0].startswith("ffn"))
            if final is not None:
                early_io(s, final)
            for j in range(NFF):
                if early and j == 12:
                    early_stats(s)
                pi = acq(("gu", k, fi, j))
                slot = ws(pi)
                bg = nb(s)
                bu = nb(s)
                xr = [R("XN", s, kc) for kc in range(NCH)]
                B.mm_group(PS[bg], [(W[:, slot, kc * 128:(kc + 1) * 128], XN[s][:, kc, :]) for kc in range(NCH)],
                           reads=[R("W", slot)] + xr, writes=[R("PS", bg)])
                B.mm_group(PS[bu], [(W[:, slot, 1024 + kc * 128:1024 + (kc + 1) * 128], XN[s][:, kc, :])
                                    for kc in range(NCH)],
                           reads=[R("W", slot)] + xr, writes=[R("PS", bu)])
                rel(pi)
                g = sgi[s] % 2
                sgi[s] += 1
                B.op("act", lambda e: e.activation(out=SG[s][:, g, :], in_=PS[bg], func=AF.Silu),
                     reads=[R("PS", bg)], writes=[R("SG", s, g)], tab="silu")
                B.op("dve", lambda e: e.tensor_tensor(out=A[:, j, :], in0=SG[s][:, g, :], in1=PS[bu], op=ALU.mult),
                     reads=[R("SG", s, g), R("PS", bu)], writes=[R("BIG", s, j)])
                if j % gy == gy - 1:
                    yield 7.6
            if early:
                early_xn(s, NF + (int(stages[0][3]) * 2 + 0) * 8)
                yield 6.0
            for i in range(NCH):
                p0 = acq(("d", k, fi, i, 0))
                p1 = acq(("d", k, fi, i, 1))
                bf = nb(s)
                pairs = []
                for f in range(NFF):
                    s_ = ws(p0) if f < 11 else ws(p1)
                    pairs.append((W[:, s_, (f % 11) * 128:(f % 11 + 1) * 128], A[:, f, :]))
                if i == 0:
                    B.mm_group_seq(PS[bf], pairs, [[R("BIG", s, f)] for f in range(NFF)],
                                   [R("W", ws(p0)), R("W", ws(p1))], [R("PS", bf)])
                else:
                    B.mm_group(PS[bf], pairs,
                               reads=[R("W", ws(p0)), R("W", ws(p1))] + [R("BIG", s, f) for f in range(NFF)],
                               writes=[R("PS", bf)])
                rel(p0)
                rel(p1)
                B.op("act", lambda e: e.activation(out=FB[s][:, i, :], in_=PS[bf], func=AF.Copy),
                     reads=[R("PS", bf)], writes=[R("FB", s, i)])
                B.op("act", lambda e: e.activation(out=SQ[s][:, i, :], in_=PS[bf], func=AF.Square),
                     reads=[R("PS", bf)], writes=[R("SQ", s, i)])
                if i % dy == dy - 1:
                    yield 5.2
            yield from norm_out(s, CG, gout, presq, final)

        def bigu(s, lo, hi):
            return [R("BIG", s, u) for u in range(lo // 256, (hi - 1) // 256 + 1)]

        lru_done = [False] * (ntiles + 1)
        akv_done = [False] * (ntiles + 1)

        def lru(s, k, presq, final=None):
            Ys = [BIG[s][:, hp * 1024:(hp + 1) * 1024].rearrange("p (c t) -> p c t", c=2) for hp in range(2)]
            PREs = [BIG[s][:, 2048 + hp * 1280:2048 + hp * 1280 + 1040].rearrange("p (c t) -> p c t", c=2)
                    for hp in range(2)]
            XBv = BIG[s][:, 4608:5632].rearrange("p (c t) -> p c t", c=2)
            XBb = SG[s]
            rYs = [bigu(s, hp * 1024, (hp + 1) * 1024) for hp in range(2)]
            rPREs = [bigu(s, 2048 + hp * 1280, 2048 + (hp + 1) * 1280) for hp in range(2)]
            rXB = bigu(s, 4608, 5632)
            rXBb = [R("SG", s, 0), R("SG", s, 1)]
            norm_in(s, NM + 0)
            yield 10.0
            if final is not None:
                early_io(s, final)
            while k > 0 and not lru_done[k - 1]:
                yield None

            def inproj(hd):
                Yv, PREv, rY, rPRE = Ys[hd % 2], PREs[hd % 2], rYs[hd % 2], rPREs[hd % 2]
                for cc in range(2):
                    c = 2 * hd + cc
                    pi = acq(("lin", k, c))
                    slot = ws(pi)
                    by = nb(s)
                    bp = nb(s)
                    xr = [R("XN", s, kc) for kc in range(NCH)]
                    B.mm_group(PS[by], [(W[:, slot, kc * 128:(kc + 1) * 128], XN[s][:, kc, :]) for kc in range(NCH)],
                               reads=[R("W", slot)] + xr, writes=[R("PS", by)])
                    B.mm_group(PS[bp], [(W[:, slot, 1024 + kc * 128:1024 + (kc + 1) * 128], XN[s][:, kc, :])
                                        for kc in range(NCH)],
                               reads=[R("W", slot)] + xr, writes=[R("PS", bp)])
                    rel(pi)
                    B.op("act", lambda e: e.activation(
                        out=Yv[:, cc, :], in_=PS[by], func=AF.Gelu_apprx_tanh,
                        bias=PV[:, BIN + c:BIN + c + 1], scale=1.0),
                        reads=[R("PS", by), R("PV")], writes=rY, tab="gelu")
                    B.op("act", lambda e: e.activation(
                        out=PREv[:, cc, 3:3 + T], in_=PS[bp], func=AF.Identity,
                        bias=PV[:, BIN + 8 + c:BIN + 8 + c + 1], scale=1.0),
                        reads=[R("PS", bp), R("PV")], writes=rPRE)
                    yield 3.8

            yield from inproj(0)
            for hd in range(4):
                Yv, PREv, rY, rPRE = Ys[hd % 2], PREs[hd % 2], rYs[hd % 2], rPREs[hd % 2]
                if hd + 1 < 4:
                    yield from inproj(hd + 1)
                for cc in range(2):
                    c = 2 * hd + cc
                    B.op("dve", lambda e, cc=cc, c=c: e.tensor_copy(out=PREv[:, cc, 0:3], in_=CARRY[:, c, 0:3]),
                         reads=[R("CARRY")], writes=rPRE, n=4)
                    B.op("act", lambda e, cc=cc, c=c: e.activation(
                        out=XBv[:, cc, :], in_=PREv[:, cc, 3:3 + T], func=AF.Identity,
                        scale=PV[:, CW + 3 * 8 + c:CW + 3 * 8 + c + 1], bias=PV[:, CB + c:CB + c + 1]),
                        reads=rPRE + [R("PV")], writes=[R("XBc", s, cc)] + bigu(s, 4608 + cc * 512, 5120 + cc * 512))
                for kk in range(3):
                    for cc in range(2):
                        c = 2 * hd + cc
                        B.op("dve", lambda e, kk=kk, cc=cc, c=c: e.scalar_tensor_tensor(
                            out=XBv[:, cc, :], in0=PREv[:, cc, kk:kk + T],
                            scalar=PV[:, CW + kk * 8 + c:CW + kk * 8 + c + 1],
                            in1=XBv[:, cc, :], op0=ALU.mult, op1=ALU.add),
                            reads=rPRE + [R("XBc", s, cc), R("PV")], writes=[R("XBc", s, cc)], f=2.0)
                for cc in range(2):
                    c = 2 * hd + cc
                    B.op("dve", lambda e, cc=cc, c=c: e.tensor_copy(out=CARRY[:, c, 0:3], in_=PREv[:, cc, T:T + 3]),
                         reads=rPRE, writes=[R("CARRY")], n=4)
                    B.op("act", lambda e, cc=cc: e.activation(out=XBb[:, cc, :], in_=XBv[:, cc, :], func=AF.Copy),
                         reads=[R("XBc", s, cc)] + bigu(s, 4608 + cc * 512, 5120 + cc * 512), writes=[rXBb[cc]])
                yield 8.0
                pi = acq(("lg", k, hd))
                slot = ws(pi)
                for oc in range(2):
                    c = 2 * hd + oc
                    br = nb(s)
                    bi = nb(s)
                    B.mm_group(PS[br], [(W[:, slot, k2 * 256 + oc * 128:k2 * 256 + (oc + 1) * 128], XBb[:, k2, :])
                                        for k2 in range(2)],
                               reads=[R("W", slot)] + rXBb, writes=[R("PS", br)])
                    B.mm_group(PS[bi], [(W[:, slot, 512 + k2 * 256 + oc * 128:512 + k2 * 256 + (oc + 1) * 128],
                                         XBb[:, k2, :]) for k2 in range(2)],
                               reads=[R("W", slot)] + rXBb, writes=[R("PS", bi)])
                    B.op("act", lambda e: e.activation(
                        out=FB[s][:, oc, :], in_=PS[br], func=AF.Sigmoid, bias=PV[:, BA + c:BA + c + 1], scale=1.0),
                        reads=[R("PS", br), R("PV")], writes=[R("FB", s, oc)], tab="sig")
                    B.op("act", lambda e: e.activation(
                        out=FB[s][:, 4 + oc, :], in_=PS[bi], func=AF.Sigmoid, bias=PV[:, BI + c:BI + c + 1], scale=1.0),
                        reads=[R("PS", bi), R("PV")], writes=[R("FB", s, 4 + oc)], tab="sig")
                rel(pi)
                yield 2.0
                Rr = [FB[s][:, oc, :] for oc in range(2)]
                Mm = [FB[s][:, 2 + oc, :] for oc in range(2)]
                Ii = [FB[s][:, 4 + oc, :] for oc in range(2)]
                rR = [[R("FB", s, oc)] for oc in range(2)]
                rM = [[R("FB", s, 2 + oc)] for oc in range(2)]
                rI = [[R("FB", s, 4 + oc)] for oc in range(2)]
                cs = [2 * hd + oc for oc in range(2)]
                for oc in range(2):
                    B.op("act", lambda e, oc=oc: e.activation(out=Mm[oc], in_=Rr[oc], func=AF.Exp,
                                                              scale=CL[:, 16 + cs[oc]:16 + cs[oc] + 1]),
                         reads=rR[oc] + [R("CL")], writes=rM[oc], tab="exp")
                    B.op("act", lambda e, oc=oc: e.activation(out=Rr[oc], in_=Rr[oc], func=AF.Exp,
                                                              scale=CL[:, 8 + cs[oc]:8 + cs[oc] + 1]),
                         reads=rR[oc] + [R("CL")], writes=rR[oc], tab="exp")
                for oc in range(2):
                    B.op("act", lambda e, oc=oc: e.activation(out=Mm[oc], in_=Mm[oc], func=AF.Sqrt,
                                                              bias=ONEC[:, 0:1], scale=-1.0),
                         reads=rM[oc], writes=rM[oc], tab="sqrt")
                for oc in range(2):
                    B.op("dve", lambda e, oc=oc: e.tensor_tensor(out=Ii[oc], in0=Ii[oc], in1=Mm[oc], op=ALU.mult),
                         reads=rI[oc] + rM[oc], writes=rI[oc], f=2.0)
                for oc in range(2):
                    B.op("dve", lambda e, oc=oc: e.tensor_tensor(out=Ii[oc], in0=Ii[oc], in1=XBv[:, oc, :], op=ALU.mult),
                         reads=rI[oc] + [R("XBc", s, oc)] + bigu(s, 4608 + oc * 512, 5120 + oc * 512),
                         writes=rI[oc], f=2.0)
                yield 4.0
                for oc in range(2):
                    B.op("dve", lambda e, oc=oc: e.tensor_tensor_scan(
                        out=Mm[oc], data0=Rr[oc], data1=Ii[oc], initial=HST[:, cs[oc]:cs[oc] + 1],
                        op0=ALU.mult, op1=ALU.add),
                        reads=rR[oc] + rI[oc] + [R("HST", cs[oc])], writes=rM[oc], f=2.0)
                for oc in range(2):
                    B.op("dve", lambda e, oc=oc: e.tensor_copy(out=HST[:, cs[oc]:cs[oc] + 1], in_=Mm[oc][:, T - 1:T]),
                         reads=rM[oc], writes=[R("HST", cs[oc])], n=1)
                for oc in range(2):
                    B.op("dve", lambda e, oc=oc: e.tensor_tensor(out=HY[s][:, cs[oc], :], in0=Mm[oc], in1=Yv[:, oc, :],
                                                                 op=ALU.mult),
                         reads=rM[oc] + rY, writes=[R("HY", s, cs[oc])], f=2.0)
                yield 4.0
            lru_done[k] = True
            yield from out_proj(s, k, "lout", BO, PV, NM + 8, presq, final)

        SCALE = 0.125
        NB_ = T // 128

        def attn(s, k, presq, final=None):
            BIGb = BIG[s][:, :].bitcast(BF16)
            QZ = BIGb[:, 0:8192].rearrange("p (c h t) -> p c h t", c=8, h=2)
            KT = BIGb[:, 8192:8192 + 128 + T]
            VP = BIGb[:, 8832:8832 + (NB_ + 1) * 256].rearrange("p (b h m) -> p b h m", b=NB_ + 1, h=2)
            rQZ = bigu(s, 0, 4096)
            rKT = bigu(s, 4096, 4416)
            rVP = bigu(s, 4416, 4416 + (NB_ + 1) * 128)
            EEs = [FB[s][:, kk, :].bitcast(BF16).rearrange("p (h q) -> p h q", h=4) for kk in range(3)]
            PTs = [FB[s][:, 3 + kk, :].bitcast(BF16).rearrange("p (a q) -> p a q", a=8) for kk in range(3)]
            rEE = [R("FB", s, kk) for kk in range(3)]
            rPT = [R("FB", s, 3 + kk) for kk in range(3)]
            norm_in(s, NM + 16)
            yield 10.0
            if final is not None:
                early_io(s, final)
            B.op("dve", lambda e: e.memset(QZ[64:128, :, 0, :], 0.0), writes=rQZ, n=4096)
            B.op("dve", lambda e: e.memset(QZ[0:64, :, 1, :], 0.0), writes=rQZ, n=4096)
            B.op("dve", lambda e: e.memset(VP[:, 1:NB_ + 1, 0, 64:128], 0.0), writes=rVP, n=256)
            B.op("dve", lambda e: e.memset(VP[:, 1:NB_ + 1, 1, 0:64], 0.0), writes=rVP, n=256)
            yield 9.0
            while k > 0 and not akv_done[k - 1]:
                yield None
            B.op("dve", lambda e: e.tensor_copy(out=KT[:, 0:128], in_=KC[:, :]), reads=[R("KC")], writes=rKT)
            B.op("dve", lambda e: e.tensor_copy(out=VP[:, 0, :, :], in_=VC[:, :, :]), reads=[R("VC")], writes=rVP)
            pi = acq(("akv", k))
            slot = ws(pi)
            b = nb(s)
            B.mm_group(PS[b], [(W[:, slot, kc * 128:(kc + 1) * 128], XN[s][:, kc, :]) for kc in range(NCH)],
                       reads=[R("W", slot)] + [R("XN", s, kc) for kc in range(NCH)], writes=[R("PS", b)])
            B.op("act", lambda e: e.activation(out=KT[:, 128:128 + T], in_=PS[b], func=AF.Identity,
                                               bias=PV[:, BK:BK + 1], scale=1.0),
                 reads=[R("PS", b), R("PV")], writes=rKT)
            b = nb(s)
            for blk in range(NB_):
                B.mm_group(PS[b][:, blk * 128:(blk + 1) * 128],
                           [(XN[s][:, kc, blk * 128:(blk + 1) * 128], W[:, slot, 1024 + kc * 128:1024 + (kc + 1) * 128])
                            for kc in range(NCH)],
                           reads=[R("W", slot)] + [R("XN", s, kc) for kc in range(NCH)], writes=[R("PS", b)])
            rel(pi)
            PSv = PS[b].rearrange("p (b m) -> p b m", b=NB_)
            B.op("dve", lambda e: e.tensor_tensor(
                out=VP[:, 1:NB_ + 1, 0, 0:64], in0=PSv[:, :, 0:64],
                in1=BC[:, 0:64].unsqueeze(1).broadcast_to([128, NB_, 64]), op=ALU.add),
                reads=[R("PS", b), R("BC")], writes=rVP)
            B.op("dve", lambda e: e.tensor_tensor(
                out=VP[:, 1:NB_ + 1, 1, 64:128], in0=PSv[:, :, 64:128],
                in1=BC[:, 64:128].unsqueeze(1).broadcast_to([128, NB_, 64]), op=ALU.add),
                reads=[R("PS", b), R("BC")], writes=rVP)
            B.op("dve", lambda e: e.tensor_copy(out=KC[:, :], in_=KT[:, T:T + 128]), reads=rKT, writes=[R("KC")])
            B.op("dve", lambda e: e.tensor_copy(out=VC[:, :, :], in_=VP[:, NB_, :, :]), reads=rVP, writes=[R("VC")])
            akv_done[k] = True
            yield 4.0
            for pq in range(4):
                pi = acq(("aq", k, pq))
                slot = ws(pi)
                for c2 in range(2):
                    c = 2 * pq + c2
                    b = nb(s)
                    B.mm_group(PS[b],
                               [(W[:, slot, (c2 * 8 + kc) * 128:(c2 * 8 + kc + 1) * 128], XN[s][:, kc, :])
                                for kc in range(NCH)],
                               reads=[R("W", slot)] + [R("XN", s, kc) for kc in range(NCH)], writes=[R("PS", b)])
                    B.op("act", lambda e: e.activation(
                        out=QZ[0:64, c, 0, :], in_=PS[b][0:64, :], func=AF.Identity,
                        bias=PV[0:64, BQ + c:BQ + c + 1], scale=1.0),
                        reads=[R("PS", b), R("PV")], writes=bigu(s, c * 512, (c + 1) * 512))
                    B.op("act", lambda e: e.activation(
                        out=QZ[64:128, c, 1, :], in_=PS[b][64:128, :], func=AF.Identity,
                        bias=PV[64:128, BQ + c:BQ + c + 1], scale=1.0),
                        reads=[R("PS", b), R("PV")], writes=bigu(s, c * 512, (c + 1) * 512))
                rel(pi)
                yield 3.8

            steps = [(n, g) for n in range(NB_) for g in range(4)]
            NSTEP = len(steps)

            def phaseA(i):
                n, g = steps[i]
                kk = i % 3
                b = nb2(s)
                mk = MASK0 if (k == 0 and n == 0) else MASK
                for hp in range(2):
                    c = 2 * g + hp
                    for hf in range(2):
                        B.mm_group(PSALL[:, (b + hp) * 512 + hf * 256:(b + hp) * 512 + (hf + 1) * 256],
                                   [(QZ[:, c, hf, n * 128:(n + 1) * 128], KT[:, n * 128:n * 128 + 256]),
                                    (IDENT[:, :], mk[:, :])],
                                   reads=bigu(s, c * 512, (c + 1) * 512) + rKT + [R("IDENT"), R("MASK"), R("MASK0")],
                                   writes=[R("PS", b + hp)])
                for hp in range(2):
                    bank = b + hp
                    mxc = 0 if hp == 0 else 14
                    ngc = 1 if hp == 0 else 15
                    B.op("dve", lambda e, bank=bank, mxc=mxc: e.tensor_reduce(
                        out=ST[s][:, kk, mxc:mxc + 1], in_=PS[bank], axis=AX.X, op=ALU.max),
                        reads=[R("PS", bank)], writes=[R("ST", s, kk, 0, hp)], n=512)
                    B.op("dve", lambda e, mxc=mxc, ngc=ngc: e.tensor_scalar(
                        out=ST[s][:, kk, ngc:ngc + 1], in0=ST[s][:, kk, mxc:mxc + 1], scalar1=-SCALE,
                        scalar2=NSMAX[:, 0:1], op0=ALU.mult, op1=ALU.min),
                        reads=[R("ST", s, kk, 0, hp), R("NSMAX")], writes=[R("ST", s, kk, 1, hp)], n=4)
                    for hf in range(2):
                        hh = 2 * hp + hf
                        B.op("act", lambda e, bank=bank, hf=hf, hh=hh, ngc=ngc: e.activation(
                            out=EEs[kk][:, hh, :], in_=PSALL[:, bank * 512 + hf * 256:bank * 512 + (hf + 1) * 256],
                            func=AF.Exp, bias=ST[s][:, kk, ngc:ngc + 1], scale=SCALE,
                            accum_out=ST[s][:, kk, 2 + hh:3 + hh]),
                            reads=[R("PS", bank), R("ST", s, kk, 1, hp)], writes=[rEE[kk], R("ST", s, kk, 2)],
                            n=400, tab="exp")
                    B.op("act", lambda e, hp=hp, ngc=ngc: e.activation(
                        out=ST[s][:, kk, 6 + 2 * hp:8 + 2 * hp], in_=BC[:, 128 + 4 * g + 2 * hp:128 + 4 * g + 2 * hp + 2],
                        func=AF.Exp, bias=ST[s][:, kk, ngc:ngc + 1], scale=1.0),
                        reads=[R("ST", s, kk, 1, hp), R("BC")], writes=[R("ST", s, kk, 3)], n=4, tab="exp")

            def phaseB(i):
                n, g = steps[i]
                kk = i % 3
                B.op("dve", lambda e: e.tensor_tensor(out=ST[s][:, kk, 10:14], in0=ST[s][:, kk, 2:6],
                                                      in1=ST[s][:, kk, 6:10], op=ALU.add),
                     reads=[R("ST", s, kk, 2), R("ST", s, kk, 3)], writes=[R("ST", s, kk, 4)], n=4)
                B.op("dve", lambda e: e.reciprocal(out=ST[s][:, kk, 10:14], in_=ST[s][:, kk, 10:14]),
                     reads=[R("ST", s, kk, 4)], writes=[R("ST", s, kk, 4)], n=32)
                for hh in range(4):
                    B.op("dve", lambda e, hh=hh: e.tensor_scalar(
                        out=EEs[kk][:, hh, :], in0=EEs[kk][:, hh, :], scalar1=ST[s][:, kk, 10 + hh:11 + hh],
                        scalar2=None, op0=ALU.mult),
                        reads=[rEE[kk], R("ST", s, kk, 4)], writes=[rEE[kk]], n=256)

            def phaseB2(i):
                n, g = steps[i]
                kk = i % 3
                bt = nb(s)
                PTb = PS[bt].bitcast(BF16)
                def emit_tr():
                    ins = None
                    for hh in range(4):
                        for k2 in range(2):
                            a_ = hh * 2 + k2
                            ins = nc.tensor.transpose(PTb[:, a_ * 128:(a_ + 1) * 128],
                                                      EEs[kk][:, hh, k2 * 128:(k2 + 1) * 128], IDENT[:, :])
                    return ins
                B.pe_manual(emit_tr, [rEE[kk], R("IDENT")], [R("PS", bt)], 8 * 0.08)
                B.op("act", lambda e: e.activation(out=PTs[kk], in_=PTb.rearrange("p (a q) -> p a q", a=8),
                                                   func=AF.Copy),
                     reads=[R("PS", bt)], writes=[rPT[kk]], n=1024)

            def phaseC(i):
                n, g = steps[i]
                kk = i % 3
                bo = nb(s)
                for ci in range(2):
                    B.mm_group(PS[bo][:, ci * 128:(ci + 1) * 128],
                               [(VP[:, n + k2, hf, :], PTs[kk][:, (ci * 2 + hf) * 2 + k2, :])
                                for hf in range(2) for k2 in range(2)],
                               reads=rVP + [rPT[kk]], writes=[R("PS", bo)])
                B.op("act", lambda e: e.activation(
                    out=HY[s][:, 2 * g:2 * g + 2, n * 128:(n + 1) * 128],
                    in_=PS[bo][:, 0:256].rearrange("p (c q) -> p c q", c=2), func=AF.Copy),
                    reads=[R("PS", bo)], writes=[R("HY", s, 2 * g), R("HY", s, 2 * g + 1)], n=256)

            for i in range(NSTEP + 4):
                if i < NSTEP:
                    phaseA(i)
                if 0 <= i - 1 < NSTEP:
                    phaseB(i - 1)
                if 0 <= i - 2 < NSTEP:
                    phaseB2(i - 2)
                if 0 <= i - 4 < NSTEP:
                    phaseC(i - 4)
                if i % ay == ay - 1:
                    yield 3.6
            yield from out_proj(s, k, "ao", AO, PV, NM + 24, presq, final)

        def chain(s):
            for k in range(s, ntiles, NS_):
                if k < NS_:
                    for i in range(NCH):
                        B.dma("sp", xsem[s][i], X[s][:, i, :], xT3[:, i, k * T:(k + 1) * T],
                              writes=[R("X", s, i)], nbytes=1 << 18)
                yield 1.0
                sq_ready[s] = False
                for si, st in enumerate(stages):
                    last = si + 1 == len(stages)
                    presq = not last
                    final = k if last else None
                    if st.startswith("ffn"):
                        yield from ffn(s, k, int(st[3]), presq, final)
                    elif st == "lru":
                        yield from lru(s, k, presq, final)
                    elif st == "attn":
                        yield from attn(s, k, presq, final)
                yield 1.0

        issue()
        gens = [chain(s) for s in range(NS_)]
        start_off = [0.0, offset]
        alive = [True, True]
        while any(alive):
            order = sorted([s for s in range(NS_) if alive[s]], key=lambda s: max(start_off[s], B.stime[s]))
            progressed = False
            for s in order:
                B.cur = s
                try:
                    r = next(gens[s])
                except StopIteration:
                    alive[s] = False
                    progressed = True
                    break
                if r is None:
                    continue
                progressed = True
                break
            assert progressed, "scheduler deadlock"
        for s in range(NS_):
            for i in range(NCH):
                B._wait("sp", osem[s][i], osem[s][i].v)
                B._wait("sp", asem[s][i], asem[s][i].v)
    return nc


def _tile_kn(w):
    k, n = w.shape
    return w.reshape(k // 128, 128, n).transpose(1, 0, 2)


def _cols(v):
    return np.ascontiguousarray(v.reshape(-1, 128).T)


def prepare_shared(inp):
    f = np.float32
    g = {k: np.asarray(v, dtype=f) for k, v in inp.items() if k != "x"}
    wgu = np.empty((4, NFF, 128, 2048), f)
    wdn = np.empty((4, 16, 128, 1408), f)
    for l in range(2):
        for w in range(2):
            fi = l * 2 + w
            wg = _tile_kn(g["ffn_w_gate"][l, w]).reshape(128, 8, NFF, 128).transpose(2, 0, 1, 3)
            wu = _tile_kn(g["ffn_w_up"][l, w]).reshape(128, 8, NFF, 128).transpose(2, 0, 1, 3)
            wgu[fi, :, :, 0:1024] = wg.reshape(NFF, 128, 1024)
            wgu[fi, :, :, 1024:2048] = wu.reshape(NFF, 128, 1024)
            wd = _tile_kn(g["ffn_w_down"][l, w])
            wd = wd.reshape(128, 2, 11, 8, 128).transpose(3, 1, 0, 2, 4)
            wdn[fi] = wd.reshape(16, 128, 1408)
    wmix = np.zeros((25, 128, 2048), f)
    win = _tile_kn(g["lru_w_in"][0])
    for c in range(8):
        wmix[WM_LIN + c, :, 0:1024] = win[:, :, c * 128:(c + 1) * 128].reshape(128, 1024)
        wmix[WM_LIN + c, :, 1024:2048] = win[:, :, 1024 + c * 128:1024 + (c + 1) * 128].reshape(128, 1024)
    for hd in range(4):
        wmix[WM_LG + hd, :, 0:512] = _tile_kn(g["lru_w_a"][0, hd]).reshape(128, 512)
        wmix[WM_LG + hd, :, 512:1024] = _tile_kn(g["lru_w_i"][0, hd]).reshape(128, 512)
    wout = _tile_kn(g["lru_w_out"][0])
    for ip in range(4):
        for i2 in range(2):
            i = 2 * ip + i2
            wmix[WM_LOUT + ip, :, i2 * 1024:(i2 + 1) * 1024] = wout[:, :, i * 128:(i + 1) * 128].reshape(128, 1024)
    wqkv = g["attn_w_qkv"][0]
    qcols = np.concatenate([np.r_[c * 64:(c + 1) * 64, (8 + c) * 64:(9 + c) * 64] for c in range(8)])
    wq = _tile_kn(wqkv[:, qcols])
    for pq in range(4):
        for c2 in range(2):
            c = 2 * pq + c2
            wmix[WM_AQ + pq, :, c2 * 1024:(c2 + 1) * 1024] = wq[:, :, c * 128:(c + 1) * 128].reshape(128, 1024)
    wmix[WM_AKV, :, 0:1024] = _tile_kn(wqkv[:, 1024:1152]).reshape(128, 1024)
    wmix[WM_AKV, :, 1024:2048] = _tile_kn(wqkv[:, 1152:1280]).reshape(128, 1024)
    wo = _tile_kn(g["attn_w_o"][0][qcols, :])
    for ip in range(4):
        for i2 in range(2):
            i = 2 * ip + i2
            wmix[WM_AO + ip, :, i2 * 1024:(i2 + 1) * 1024] = wo[:, :, i * 128:(i + 1) * 128].reshape(128, 1024)
    pv = np.zeros((128, NPV), f)
    pv[:, NF:NF + 64] = _cols(g["norm_ffn"].reshape(-1))
    pv[:, NM:NM + 32] = _cols(g["norm_mix"].reshape(-1))
    pv[:, BIN:BIN + 16] = _cols(g["lru_b_in"][0])
    pv[:, CW:CW + 32] = _cols(g["lru_conv_w"][0].reshape(-1))
    pv[:, CB:CB + 8] = _cols(g["lru_conv_b"][0])
    pv[:, BA:BA + 8] = _cols(g["lru_b_a"][0])
    pv[:, BI:BI + 8] = _cols(g["lru_b_i"][0])
    pv[:, LAM:LAM + 8] = _cols(g["lru_lambda"][0])
    pv[:, BO:BO + 8] = _cols(g["lru_b_out"][0])
    bqkv = g["attn_b_qkv"][0]
    pv[:, BQ:BQ + 8] = _cols(bqkv[qcols])
    pv[:, BK:BK + 1] = _cols(bqkv[1024:1152])
    pv[:, AO:AO + 8] = _cols(g["attn_b_o"][0])
    bc = np.zeros((128, 144), f)
    bc[:, 0:128] = np.broadcast_to(bqkv[1152:1280], (128, 128))
    horder = np.array([hf * 8 + c for c in range(8) for hf in range(2)])
    bc[:, 128:144] = np.broadcast_to(g["attn_sinks"][0][horder], (128, 16))
    return dict(wgu=wgu.reshape(4 * NFF, 128, 2048), wdn=wdn.reshape(64, 128, 1408), wmix=wmix, pvec=pv, bcast=bc)


_PROG = {}


def kernel(**inputs):
    x = np.asarray(inputs["x"], dtype=np.float32)
    shared = prepare_shared(inputs)
    if "full" not in _PROG:
        _PROG["full"] = build_program()
    nc = _PROG["full"]
    in_maps = []
    for b in range(NCORES):
        m = dict(shared)
        m["xT"] = np.ascontiguousarray(x[b].T)
        in_maps.append(m)
    res = run_bass_kernel_spmd(nc, in_maps, core_ids=list(range(NCORES)))
    out = np.stack([np.ascontiguousarray(r["outT"].T) for r in res.results], axis=0)
    return out.astype(np.float32)
```

```python
import numpy as np
from contextlib import ExitStack
import concourse.bass as bass
import concourse.mybir as mybir
from concourse.bass_utils import run_bass_kernel_spmd

F32 = mybir.dt.float32
BF16 = mybir.dt.bfloat16
AF = mybir.ActivationFunctionType
ALU = mybir.AluOpType
AX = mybir.AxisListType

D = 1024
SEQ = 4096
DFF = 2816
NCH = 8
NFF = 22
T = 512
NT = SEQ // T
NBLK = T // 128
EPS = 1e-6
NCORES = 8
NWS = 8
WCOLS = 2048
ALL_STAGES = ("ffn0", "lru", "ffn1", "ffn2", "attn", "ffn3")

NF, NM, BIN, CW, CB, BA, BI, LAM, BO, BQ, BK, AO, NPV = 0, 64, 96, 112, 144, 152, 160, 168, 176, 184, 192, 200, 208
WM_LIN, WM_LG, WM_LOUT, WM_AQ, WM_AKV, WM_AO = 0, 8, 12, 16, 20, 21


class Sem:
    def __init__(self, nc, es, name):
        self.h = es.enter_context(nc.semaphore(name))
        self.v = 0


class Res:
    __slots__ = ("lw", "rd")

    def __init__(self):
        self.lw = None
        self.rd = {}


class Builder:
    BASE = dict(pe=0.01, act=0.25, dve=0.12, pool=0.5, sp=0.1)
    RATE = dict(pe=2370.0, act=1400.0, dve=960.0, pool=300.0, sp=1e9)
    HOP = 0.3

    def __init__(self, nc, es):
        self.nc = nc
        self.es = es
        self.eng = dict(pe=nc.tensor, act=nc.scalar, dve=nc.vector, pool=nc.gpsimd, sp=nc.sync)
        self.esem = {k: Sem(nc, es, "s_" + k) for k in self.eng}
        self.waited = {k: {} for k in self.eng}
        self.res = {}
        self.etime = {k: 0.0 for k in self.eng}
        self.ttime = {}
        self.dma_free = 0.0
        self.cur = 0
        self.stime = [0.0, 0.0]
        self.acttab = None
        self.ntab = 0

    def R(self, *key):
        r = self.res.get(key)
        if r is None:
            r = self.res[key] = Res()
        return r

    def _wait(self, en, sem, val):
        w = self.waited[en]
        if w.get(sem, 0) >= val:
            return
        self.eng[en].wait_ge(sem.h, val)
        w[sem] = val

    def _deps(self, en, reads, writes):
        need = {}
        for r in reads:
            if r.lw is not None:
                s, v = r.lw
                if need.get(s, 0) < v:
                    need[s] = v
        for r in writes:
            if r.lw is not None:
                s, v = r.lw
                if need.get(s, 0) < v:
                    need[s] = v
            for s, v in r.rd.items():
                if need.get(s, 0) < v:
                    need[s] = v
        ready = 0.0
        for s, v in need.items():
            self._wait(en, s, v)
            t = self.ttime.get((s, v), 0.0)
            if t > ready:
                ready = t
        return ready

    def _reg(self, tok, reads, writes):
        s, v = tok
        for r in writes:
            r.lw = tok
            r.rd = {}
        for r in reads:
            if r.rd.get(s, 0) < v:
                r.rd[s] = v

    def _time(self, en, ready, dur, tok):
        start = max(self.etime[en], ready + self.HOP)
        end = start + dur
        self.etime[en] = end
        self.ttime[tok] = end
        if end > self.stime[self.cur]:
            self.stime[self.cur] = end

    def op(self, en, fn, reads=(), writes=(), n=512, f=1.0, tab=None):
        ready = self._deps(en, reads, writes)
        if tab is not None and tab != self.acttab:
            self.acttab = tab
            self.etime[en] = max(self.etime[en], ready) + 1.28
            self.ntab += 1
        ins = fn(self.eng[en])
        sem = self.esem[en]
        sem.v += 1
        ins.then_inc(sem.h, 1)
        tok = (sem, sem.v)
        self._reg(tok, reads, writes)
        feff = 1.0 + 0.35 * (f - 1.0) if f <= 2.0 else f
        self._time(en, ready, self.BASE[en] + feff * n / self.RATE[en], tok)
        return tok

    def dma(self, en, sem, out, in_, reads=(), writes=(), nbytes=1 << 20, accum=False):
        ready = self._deps(en, reads, writes)
        if accum:
            ins = self.eng[en].dma_start(out=out, in_=in_, accum_op=ALU.add)
        else:
            ins = self.eng[en].dma_start(out=out, in_=in_)
        sem.v += 16
        ins.then_inc(sem.h, 16)
        tok = (sem, sem.v)
        self._reg(tok, reads, writes)
        if en == "pool":
            self.ttime[tok] = 0.0
            return tok
        issue_end = max(self.etime[en], ready + self.HOP) + 0.1
        self.etime[en] = issue_end
        xs = max(issue_end, self.dma_free)
        self.dma_free = xs + nbytes / 300e3
        self.ttime[tok] = self.dma_free + 2.0
        return tok

    def mm_group(self, out_ap, pairs, reads, writes):
        ready = self._deps("pe", reads, writes)
        n = len(pairs)
        ins = None
        dur = 0.0
        for i, (l, r) in enumerate(pairs):
            ins = self.nc.tensor.matmul(out_ap, lhsT=l, rhs=r, start=(i == 0), stop=(i == n - 1))
            dur += max(r.free_size(), 96) / self.RATE["pe"] + 0.005
        sem = self.esem["pe"]
        sem.v += 1
        ins.then_inc(sem.h, 1)
        tok = (sem, sem.v)
        self._reg(tok, reads, writes)
        self._time("pe", ready, dur, tok)
        return tok

    def mm_group_seq(self, out_ap, pairs, per_pair_reads, common_reads, writes):
        ready0 = self._deps("pe", common_reads, writes)
        n = len(pairs)
        ins = None
        allreads = list(common_reads)
        for i, (l, r) in enumerate(pairs):
            ri = self._deps("pe", per_pair_reads[i], ())
            allreads += list(per_pair_reads[i])
            ins = self.nc.tensor.matmul(out_ap, lhsT=l, rhs=r, start=(i == 0), stop=(i == n - 1))
            start = max(self.etime["pe"], max(ready0, ri) + self.HOP)
            self.etime["pe"] = start + max(r.free_size(), 96) / self.RATE["pe"] + 0.005
        sem = self.esem["pe"]
        sem.v += 1
        ins.then_inc(sem.h, 1)
        tok = (sem, sem.v)
        self._reg(tok, allreads, writes)
        end = self.etime["pe"]
        self.ttime[tok] = end
        if end > self.stime[self.cur]:
            self.stime[self.cur] = end
        return tok

    def pe_manual(self, emit, reads, writes, dur):
        ready = self._deps("pe", reads, writes)
        ins = emit()
        sem = self.esem["pe"]
        sem.v += 1
        ins.then_inc(sem.h, 1)
        tok = (sem, sem.v)
        self._reg(tok, reads, writes)
        self._time("pe", ready, dur, tok)
        return tok


def build_program(stages=ALL_STAGES, ntiles=NT, plan_keys=None, offset=255.0, gy=2, dy=1, ay=1):
    if plan_keys is None:
        rec = []
        build_program(stages, ntiles, plan_keys=rec, offset=offset, gy=gy, dy=dy, ay=ay)
        plan_keys = tuple(rec)
        recording = False
    else:
        recording = isinstance(plan_keys, list)
    nc = bass.Bass("TRN2", target_bir_lowering=False)
    xT = nc.dram_tensor("xT", [D, SEQ], F32, kind="ExternalInput").ap()
    wgu = nc.dram_tensor("wgu", [4 * NFF, 128, 2048], F32, kind="ExternalInput").ap()
    wdn = nc.dram_tensor("wdn", [4 * 16, 128, 1408], F32, kind="ExternalInput").ap()
    wmix = nc.dram_tensor("wmix", [25, 128, 2048], F32, kind="ExternalInput").ap()
    pvec = nc.dram_tensor("pvec", [128, NPV], F32, kind="ExternalInput").ap()
    bcast = nc.dram_tensor("bcast", [128, 144], F32, kind="ExternalInput").ap()
    outT = nc.dram_tensor("outT", [D, SEQ], F32, kind="ExternalOutput").ap()
    xT3 = xT.rearrange("(kc p) t -> p kc t", p=128)
    outT3 = outT.rearrange("(kc p) t -> p kc t", p=128)

    with ExitStack() as es:
        def sb(name, shape, dt):
            return es.enter_context(nc.sbuf_tensor(name, shape, dt))

        NS_ = 2
        X = [sb(f"X{s}", [128, NCH, T], F32) for s in range(NS_)]
        XN = [sb(f"XN{s}", [128, NCH, T], BF16) for s in range(NS_)]
        BIG = [sb(f"BIG{s}", [128, 5632], F32) for s in range(NS_)]
        HY = [sb(f"HY{s}", [128, NCH, T], BF16) for s in range(NS_)]
        FB = [sb(f"FB{s}", [128, NCH, T], F32) for s in range(NS_)]
        SQ = [sb(f"SQ{s}", [128, NCH, T], BF16) for s in range(NS_)]
        RT = [sb(f"RT{s}", [128, T], F32) for s in range(NS_)]
        RS = [sb(f"RS{s}", [128, T], F32) for s in range(NS_)]
        SG = [sb(f"SG{s}", [128, 2, T], BF16) for s in range(NS_)]
        ST = [sb(f"ST{s}", [128, 3, 16], F32) for s in range(NS_)]
        W = sb("W", [128, NWS, WCOLS], BF16)
        PV = sb("PV", [128, NPV], F32)
        BC = sb("BC", [128, 144], F32)
        CG = sb("CG", [128, 64], F32)
        CL = sb("CL", [128, 24], F32)
        NSINK = sb("NSINK", [128, 16], F32)
        NSMAX = sb("NSMAX", [128, 1], F32)
        ONES = sb("ONES", [128, 128], BF16)
        IDENT = sb("IDENT", [128, 128], BF16)
        MASK = sb("MASK", [128, 256], BF16)
        MASK0 = sb("MASK0", [128, 256], BF16)
        MTMP = sb("MTMP", [128, 256], F32)
        HST = sb("HST", [128, 8], F32)
        CARRY = sb("CARRY", [128, 8, 4], F32)
        KC = sb("KC", [128, 128], BF16)
        VC = sb("VC", [128, 2, 128], BF16)
        EPSC = sb("EPSC", [128, 1], F32)
        ONEC = sb("ONEC", [128, 1], F32)
        PSALL = es.enter_context(nc.psum_tensor("psall", [128, 8 * 512], F32))
        PS = [PSALL[:, i * 512:(i + 1) * 512] for i in range(8)]

        B = Builder(nc, es)
        R = B.R
        wsem = [Sem(nc, es, f"w{i}") for i in range(NWS)]
        xsem = [[Sem(nc, es, f"xl{s}_{i}") for i in range(NCH)] for s in range(NS_)]
        osem = [[Sem(nc, es, f"xo{s}_{i}") for i in range(NCH)] for s in range(NS_)]
        asem = [[Sem(nc, es, f"xa{s}_{i}") for i in range(NCH)] for s in range(NS_)]
        csem = Sem(nc, es, "cst")
        bank_i = [0, 0]

        def nb(s):
            b = 4 * s + bank_i[s] % 4
            bank_i[s] += 1
            return b

        def nb2(s):
            if bank_i[s] % 2:
                bank_i[s] += 1
            b = 4 * s + bank_i[s] % 4
            bank_i[s] += 2
            return b

        def piece_src(key):
            kind = key[0]
            if kind == "gu":
                return wgu[key[2] * NFF + key[3]], 2048
            if kind == "d":
                return wdn[key[2] * 16 + key[3] * 2 + key[4]], 1408
            if kind == "lin":
                return wmix[WM_LIN + key[2]], 2048
            if kind == "lg":
                return wmix[WM_LG + key[2]], 1024
            if kind == "lout":
                return wmix[WM_LOUT + key[2]], 2048
            if kind == "aq":
                return wmix[WM_AQ + key[2]], 2048
            if kind == "akv":
                return wmix[WM_AKV], 2048
            if kind == "ao":
                return wmix[WM_AO + key[2]], 2048
            raise KeyError(key)

        wstate = dict(issued=0, acq=0)
        released = []

        def issue_one():
            i = wstate["issued"]
            slot = i % NWS
            src, ncols = piece_src(plan_keys[i])
            B.dma("pool", wsem[slot], W[:, slot, 0:ncols], src[:, 0:ncols], writes=[R("W", slot)],
                  nbytes=128 * ncols * 4)
            wstate["issued"] += 1

        def issue():
            if recording:
                return
            while wstate["issued"] < len(plan_keys):
                i = wstate["issued"]
                if i >= NWS and not (i - NWS < len(released) and released[i - NWS]):
                    break
                issue_one()

        def acq(key):
            i = wstate["acq"]
            wstate["acq"] += 1
            released.append(False)
            if recording:
                plan_keys.append(key)
                assert i < NWS or released[i - NWS], "too many weight pieces open"
                issue_one()
            else:
                assert plan_keys[i] == key, (plan_keys[i], key)
                assert i < wstate["issued"], "weight piece not prefetched (too many open)"
            return i

        def rel(i):
            released[i] = True
            issue()

        def ws(i):
            return i % NWS

        B.dma("sp", csem, PV[:, :], pvec[:, :], writes=[R("PV")])
        B.dma("sp", csem, BC[:, :], bcast[:, :], writes=[R("BC")])
        for en in ("act", "dve", "pool"):
            B._wait(en, csem, csem.v)
        B.op("dve", lambda e: e.memset(ONES[:, :], 1.0), writes=[R("ONES")])
        B.op("dve", lambda e: e.memset(HST[:, :], 0.0), writes=[R("HST", c) for c in range(8)])
        B.op("dve", lambda e: e.memset(CARRY[:, :, :], 0.0), writes=[R("CARRY")])
        B.op("dve", lambda e: e.memset(KC[:, :], 0.0), writes=[R("KC")])
        B.op("dve", lambda e: e.memset(VC[:, :, :], 0.0), writes=[R("VC")])
        B.op("dve", lambda e: e.memset(MTMP[:, :], 0.0), writes=[R("MTMP")])
        B.op("dve", lambda e: e.memset(EPSC[:, :], EPS), writes=[R("EPSC")])
        B.op("dve", lambda e: e.memset(ONEC[:, :], 1.0), writes=[R("ONEC")])
        B.op("dve", lambda e: e.tensor_scalar(out=CG[:, :], in0=PV[:, NF:NF + 64], scalar1=0.5, scalar2=None,
                                              op0=ALU.mult), reads=[R("PV")], writes=[R("CG")])
        B.op("dve", lambda e: e.tensor_scalar(out=NSINK[:, :], in0=BC[:, 128:144], scalar1=-1.0, scalar2=None,
                                              op0=ALU.mult), reads=[R("BC")], writes=[R("NSINK")])
        B.op("dve", lambda e: e.tensor_reduce(out=NSMAX[:, :], in_=NSINK[:, :], axis=AX.X, op=ALU.min),
             reads=[R("NSINK")], writes=[R("NSMAX")])
        B.op("act", lambda e: e.activation(out=CL[:, 0:8], in_=PV[:, LAM:LAM + 8], func=AF.Exp, scale=-1.0),
             reads=[R("PV")], writes=[R("CL")], tab="exp")
        B.op("act", lambda e: e.activation(out=CL[:, 0:8], in_=CL[:, 0:8], func=AF.Ln, bias=1.0, scale=1.0),
             reads=[R("CL")], writes=[R("CL")], tab="exp")
        B.op("dve", lambda e: e.tensor_scalar(out=CL[:, 8:16], in0=CL[:, 0:8], scalar1=-8.0, scalar2=None,
                                              op0=ALU.mult), reads=[R("CL")], writes=[R("CL")])
        B.op("dve", lambda e: e.tensor_scalar(out=CL[:, 16:24], in0=CL[:, 0:8], scalar1=-16.0, scalar2=None,
                                              op0=ALU.mult), reads=[R("CL")], writes=[R("CL")])
        B.op("pool", lambda e: e.affine_select(out=IDENT[:, :], in_=ONES[:, :], pattern=[[-1, 128]],
                                               compare_op=ALU.is_equal, fill=0.0, base=0, channel_multiplier=1),
             reads=[R("ONES")], writes=[R("IDENT")])
        B.op("pool", lambda e: e.affine_select(out=MTMP[:, :], in_=MTMP[:, :], pattern=[[1, 256]],
                                               compare_op=ALU.is_ge, fill=-30000.0, base=-1, channel_multiplier=-1),
             reads=[R("MTMP")], writes=[R("MTMP")])
        B.op("pool", lambda e: e.affine_select(out=MASK[:, :], in_=MTMP[:, :], pattern=[[-1, 256]],
                                               compare_op=ALU.is_ge, fill=-30000.0, base=128, channel_multiplier=1),
             reads=[R("MTMP")], writes=[R("MASK")])
        B.op("pool", lambda e: e.affine_select(out=MASK0[:, :], in_=MASK[:, :], pattern=[[1, 256]],
                                               compare_op=ALU.is_ge, fill=-30000.0, base=-128, channel_multiplier=0),
             reads=[R("MASK")], writes=[R("MASK0")])
        for en in ("act", "pe"):
            B._wait(en, B.esem["dve"], B.esem["dve"].v)
        B._wait("pe", B.esem["pool"], B.esem["pool"].v)

        def norm_stats(s):
            b = nb(s)
            B.mm_group(PS[b], [(ONES[:, :], SQ[s][:, kc, :]) for kc in range(NCH)],
                       reads=[R("SQ", s, kc) for kc in range(NCH)] + [R("ONES")], writes=[R("PS", b)])
            B.op("act", lambda e: e.activation(out=RT[s][:, :], in_=PS[b], func=AF.Ln,
                                               bias=EPSC[:, 0:1], scale=1.0 / D),
                 reads=[R("PS", b)], writes=[R("RT", s)], tab="exp")
            B.op("act", lambda e: e.activation(out=PS[b], in_=RT[s][:, :], func=AF.Exp, scale=-0.5),
                 reads=[R("RT", s)], writes=[R("PS", b)], tab="exp")
            return b

        sq_ready = [False, False]

        def norm_in(s, gcol):
            if not sq_ready[s]:
                for kc in range(NCH):
                    B.op("act", lambda e, kc=kc: e.activation(out=SQ[s][:, kc, :], in_=X[s][:, kc, :], func=AF.Square),
                         reads=[R("X", s, kc)], writes=[R("SQ", s, kc)])
            sq_ready[s] = False
            b = norm_stats(s)
            for kc in range(NCH):
                B.op("dve", lambda e, kc=kc: e.scalar_tensor_tensor(
                    out=XN[s][:, kc, :], in0=X[s][:, kc, :], scalar=PV[:, gcol + kc:gcol + kc + 1],
                    in1=PS[b], op0=ALU.mult, op1=ALU.mult),
                    reads=[R("X", s, kc), R("PS", b), R("PV")], writes=[R("XN", s, kc)], f=1.0)

        def norm_out(s, gt, gcol, presq, final=None):
            b = norm_stats(s)

            def mult(i):
                B.op("dve", lambda e: e.tensor_tensor(out=FB[s][:, i, :], in0=FB[s][:, i, :],
                                                      in1=PS[b], op=ALU.mult),
                     reads=[R("FB", s, i), R("PS", b)], writes=[R("FB", s, i)], f=1.0)

            if final is None:
                mult(0)
            for i in range(NCH):
                if final is None and i + 1 < NCH:
                    mult(i + 1)
                if final is not None:
                    B.op("dve", lambda e, i=i: e.scalar_tensor_tensor(
                        out=FB[s][:, i, :], in0=FB[s][:, i, :], scalar=gt[:, gcol + i:gcol + i + 1],
                        in1=PS[b], op0=ALU.mult, op1=ALU.mult),
                        reads=[R("FB", s, i), R("PS", b), R("PV"), R("CG")], writes=[R("FB", s, i)], f=1.0)
                    B.dma("pool", asem[s][i], outT3[:, i, final * T:(final + 1) * T], FB[s][:, i, :],
                          reads=[R("FB", s, i)], writes=[R("OUT", s, i)], nbytes=1 << 18, accum=True)
                    continue
                B.op("dve", lambda e, i=i: e.scalar_tensor_tensor(
                    out=X[s][:, i, :], in0=FB[s][:, i, :], scalar=gt[:, gcol + i:gcol + i + 1],
                    in1=X[s][:, i, :], op0=ALU.mult, op1=ALU.add),
                    reads=[R("FB", s, i), R("X", s, i), R("PV"), R("CG")], writes=[R("X", s, i)], f=2.0)
                if presq:
                    B.op("act", lambda e, i=i: e.activation(out=SQ[s][:, i, :], in_=X[s][:, i, :], func=AF.Square),
                         reads=[R("X", s, i)], writes=[R("SQ", s, i)])
                if i % 2 == 1:
                    yield 1.0
            sq_ready[s] = presq

        xn_ready = [False, False]

        def early_io(s, k):
            for i in range(NCH):
                B.dma("sp", osem[s][i], outT3[:, i, k * T:(k + 1) * T], X[s][:, i, :],
                      reads=[R("X", s, i)], writes=[R("OUT", s, i)], nbytes=1 << 18)
            if k + NS_ < ntiles:
                for i in range(NCH):
                    B.dma("sp", xsem[s][i], X[s][:, i, :], xT3[:, i, (k + NS_) * T:(k + NS_ + 1) * T],
                          writes=[R("X", s, i)], nbytes=1 << 18)

        def early_stats(s):
            for kc in range(NCH):
                B.op("act", lambda e, kc=kc: e.activation(out=HY[s][:, kc, :], in_=X[s][:, kc, :], func=AF.Square),
                     reads=[R("X", s, kc)], writes=[R("HY", s, kc)])
            b = nb(s)
            B.mm_group(PS[b], [(ONES[:, :], HY[s][:, kc, :]) for kc in range(NCH)],
                       reads=[R("HY", s, kc) for kc in range(NCH)] + [R("ONES")], writes=[R("PS", b)])
            B.op("act", lambda e: e.activation(out=RT[s][:, :], in_=PS[b], func=AF.Ln,
                                               bias=EPSC[:, 0:1], scale=1.0 / D),
                 reads=[R("PS", b)], writes=[R("RT", s)], tab="exp")
            B.op("act", lambda e: e.activation(out=RT[s][:, :], in_=RT[s][:, :], func=AF.Exp, scale=-0.5),
                 reads=[R("RT", s)], writes=[R("RT", s)], tab="exp")

        def early_xn(s, gcol):
            for kc in range(NCH):
                B.op("dve", lambda e, kc=kc: e.scalar_tensor_tensor(
                    out=XN[s][:, kc, :], in0=X[s][:, kc, :], scalar=PV[:, gcol + kc:gcol + kc + 1],
                    in1=RT[s][:, :], op0=ALU.mult, op1=ALU.mult),
                    reads=[R("X", s, kc), R("RT", s), R("PV")], writes=[R("XN", s, kc)], f=2.0)
            xn_ready[s] = True

        def out_proj(s, k, kind, bcol, gt, gcol, presq, final=None):
            for ip in range(4):
                pi = acq((kind, k, ip))
                slot = ws(pi)
                for i2 in range(2):
                    i = 2 * ip + i2
                    bf = nb(s)
                    B.mm_group(PS[bf],
                               [(W[:, slot, (i2 * 8 + kc) * 128:(i2 * 8 + kc + 1) * 128], HY[s][:, kc, :])
                                for kc in range(NCH)],
                               reads=[R("W", slot)] + [R("HY", s, kc) for kc in range(NCH)],
                               writes=[R("PS", bf)])
                    B.op("act", lambda e: e.activation(
                        out=FB[s][:, i, :], in_=PS[bf], func=AF.Identity,
                        bias=PV[:, bcol + i:bcol + i + 1], scale=1.0),
                        reads=[R("PS", bf), R("PV")], writes=[R("FB", s, i)])
                    B.op("act", lambda e: e.activation(
                        out=SQ[s][:, i, :], in_=FB[s][:, i, :], func=AF.Square),
                        reads=[R("FB", s, i)], writes=[R("SQ", s, i)])
                rel(pi)
                yield 3.8
            yield from norm_out(s, gt, gcol, presq, final)

        sgi = [0, 0]

        def ffn(s, k, fi, presq, final=None):
            gin = NF + (fi * 2 + 0) * 8
            gout = (fi * 2 + 1) * 8
            A = BIG[s][:, :].bitcast(BF16).rearrange("p (j t) -> p j t", j=NFF)
            if xn_ready[s]:
                xn_ready[s] = False
            else:
                norm_in(s, gin)
            yield 10.0
            early = (final is not None and final + NS_ < ntiles and stages[0].startswith("ffn"))
            if final is not None:
                early_io(s, final)
            for j in range(NFF):
                if early and j == 12:
                    early_stats(s)
                pi = acq(("gu", k, fi, j))
                slot = ws(pi)
                bg = nb(s)
                bu = nb(s)
                xr = [R("XN", s, kc) for kc in range(NCH)]
                B.mm_group(PS[bg], [(W[:, slot, kc * 128:(kc + 1) * 128], XN[s][:, kc, :]) for kc in range(NCH)],
                           reads=[R("W", slot)] + xr, writes=[R("PS", bg)])
                B.mm_group(PS[bu], [(W[:, slot, 1024 + kc * 128:1024 + (kc + 1) * 128], XN[s][:, kc, :])
                                    for kc in range(NCH)],
                           reads=[R("W", slot)] + xr, writes=[R("PS", bu)])
                rel(pi)
                g = sgi[s] % 2
                sgi[s] += 1
                B.op("act", lambda e: e.activation(out=SG[s][:, g, :], in_=PS[bg], func=AF.Silu),
                     reads=[R("PS", bg)], writes=[R("SG", s, g)], tab="silu")
                B.op("dve", lambda e: e.tensor_tensor(out=A[:, j, :], in0=SG[s][:, g, :], in1=PS[bu], op=ALU.mult),
                     reads=[R("SG", s, g), R("PS", bu)], writes=[R("BIG", s, j)])
                if j % gy == gy - 1:
                    yield 7.6
            if early:
                early_xn(s, NF + (int(stages[0][3]) * 2 + 0) * 8)
                yield 6.0
            for i in range(NCH):
                p0 = acq(("d", k, fi, i, 0))
                p1 = acq(("d", k, fi, i, 1))
                bf = nb(s)
                pairs = []
                for f in range(NFF):
                    s_ = ws(p0) if f < 11 else ws(p1)
                    pairs.append((W[:, s_, (f % 11) * 128:(f % 11 + 1) * 128], A[:, f, :]))
                if i == 0:
                    B.mm_group_seq(PS[bf], pairs, [[R("BIG", s, f)] for f in range(NFF)],
                                   [R("W", ws(p0)), R("W", ws(p1))], [R("PS", bf)])
                else:
                    B.mm_group(PS[bf], pairs,
                               reads=[R("W", ws(p0)), R("W", ws(p1))] + [R("BIG", s, f) for f in range(NFF)],
                               writes=[R("PS", bf)])
                rel(p0)
                rel(p1)
                B.op("act", lambda e: e.activation(out=FB[s][:, i, :], in_=PS[bf], func=AF.Copy),
                     reads=[R("PS", bf)], writes=[R("FB", s, i)])
                B.op("act", lambda e: e.activation(out=SQ[s][:, i, :], in_=PS[bf], func=AF.Square),
                     reads=[R("PS", bf)], writes=[R("SQ", s, i)])
                if i % dy == dy - 1:
                    yield 5.2
            yield from norm_out(s, CG, gout, presq, final)

        def bigu(s, lo, hi):
            return [R("BIG", s, u) for u in range(lo // 256, (hi - 1) // 256 + 1)]

        lru_done = [False] * (ntiles + 1)
        akv_done = [False] * (ntiles + 1)

        def lru(s, k, presq, final=None):
            Ys = [BIG[s][:, hp * 1024:(hp + 1) * 1024].rearrange("p (c t) -> p c t", c=2) for hp in range(2)]
            PREs = [BIG[s][:, 2048 + hp * 1280:2048 + hp * 1280 + 1040].rearrange("p (c t) -> p c t", c=2)
                    for hp in range(2)]
            XBv = BIG[s][:, 4608:5632].rearrange("p (c t) -> p c t", c=2)
            XBb = SG[s]
            rYs = [bigu(s, hp * 1024, (hp + 1) * 1024) for hp in range(2)]
            rPREs = [bigu(s, 2048 + hp * 1280, 2048 + (hp + 1) * 1280) for hp in range(2)]
            rXB = bigu(s, 4608, 5632)
            rXBb = [R("SG", s, 0), R("SG", s, 1)]
            norm_in(s, NM + 0)
            yield 10.0
            if final is not None:
                early_io(s, final)
            while k > 0 and not lru_done[k - 1]:
                yield None

            def inproj(hd):
                Yv, PREv, rY, rPRE = Ys[hd % 2], PREs[hd % 2], rYs[hd % 2], rPREs[hd % 2]
                for cc in range(2):
                    c = 2 * hd + cc
                    pi = acq(("lin", k, c))
                    slot = ws(pi)
                    by = nb(s)
                    bp = nb(s)
                    xr = [R("XN", s, kc) for kc in range(NCH)]
                    B.mm_group(PS[by], [(W[:, slot, kc * 128:(kc + 1) * 128], XN[s][:, kc, :]) for kc in range(NCH)],
                               reads=[R("W", slot)] + xr, writes=[R("PS", by)])
                    B.mm_group(PS[bp], [(W[:, slot, 1024 + kc * 128:1024 + (kc + 1) * 128], XN[s][:, kc, :])
                                        for kc in range(NCH)],
                               reads=[R("W", slot)] + xr, writes=[R("PS", bp)])
                    rel(pi)
                    B.op("act", lambda e: e.activation(
                        out=Yv[:, cc, :], in_=PS[by], func=AF.Gelu_apprx_tanh,
                        bias=PV[:, BIN + c:BIN + c + 1], scale=1.0),
                        reads=[R("PS", by), R("PV")], writes=rY, tab="gelu")
                    B.op("act", lambda e: e.activation(
                        out=PREv[:, cc, 3:3 + T], in_=PS[bp], func=AF.Identity,
                        bias=PV[:, BIN + 8 + c:BIN + 8 + c + 1], scale=1.0),
                        reads=[R("PS", bp), R("PV")], writes=rPRE)
                    yield 3.8

            yield from inproj(0)
            for hd in range(4):
                Yv, PREv, rY, rPRE = Ys[hd % 2], PREs[hd % 2], rYs[hd % 2], rPREs[hd % 2]
                if hd + 1 < 4:
                    yield from inproj(hd + 1)
                for cc in range(2):
                    c = 2 * hd + cc
                    B.op("dve", lambda e, cc=cc, c=c: e.tensor_copy(out=PREv[:, cc, 0:3], in_=CARRY[:, c, 0:3]),
                         reads=[R("CARRY")], writes=rPRE, n=4)
                    B.op("act", lambda e, cc=cc, c=c: e.activation(
                        out=XBv[:, cc, :], in_=PREv[:, cc, 3:3 + T], func=AF.Identity,
                        scale=PV[:, CW + 3 * 8 + c:CW + 3 * 8 + c + 1], bias=PV[:, CB + c:CB + c + 1]),
                        reads=rPRE + [R("PV")], writes=[R("XBc", s, cc)] + bigu(s, 4608 + cc * 512, 5120 + cc * 512))
                for kk in range(3):
                    for cc in range(2):
                        c = 2 * hd + cc
                        B.op("dve", lambda e, kk=kk, cc=cc, c=c: e.scalar_tensor_tensor(
                            out=XBv[:, cc, :], in0=PREv[:, cc, kk:kk + T],
                            scalar=PV[:, CW + kk * 8 + c:CW + kk * 8 + c + 1],
                            in1=XBv[:, cc, :], op0=ALU.mult, op1=ALU.add),
                            reads=rPRE + [R("XBc", s, cc), R("PV")], writes=[R("XBc", s, cc)], f=2.0)
                for cc in range(2):
                    c = 2 * hd + cc
                    B.op("dve", lambda e, cc=cc, c=c: e.tensor_copy(out=CARRY[:, c, 0:3], in_=PREv[:, cc, T:T + 3]),
                         reads=rPRE, writes=[R("CARRY")], n=4)
                    B.op("act", lambda e, cc=cc: e.activation(out=XBb[:, cc, :], in_=XBv[:, cc, :], func=AF.Copy),
                         reads=[R("XBc", s, cc)] + bigu(s, 4608 + cc * 512, 5120 + cc * 512), writes=[rXBb[cc]])
                yield 8.0
                pi = acq(("lg", k, hd))
                slot = ws(pi)
                for oc in range(2):
                    c = 2 * hd + oc
                    br = nb(s)
                    bi = nb(s)
                    B.mm_group(PS[br], [(W[:, slot, k2 * 256 + oc * 128:k2 * 256 + (oc + 1) * 128], XBb[:, k2, :])
                                        for k2 in range(2)],
                               reads=[R("W", slot)] + rXBb, writes=[R("PS", br)])
                    B.mm_group(PS[bi], [(W[:, slot, 512 + k2 * 256 + oc * 128:512 + k2 * 256 + (oc + 1) * 128],
                                         XBb[:, k2, :]) for k2 in range(2)],
                               reads=[R("W", slot)] + rXBb, writes=[R("PS", bi)])
                    B.op("act", lambda e: e.activation(
                        out=FB[s][:, oc, :], in_=PS[br], func=AF.Sigmoid, bias=PV[:, BA + c:BA + c + 1], scale=1.0),
                        reads=[R("PS", br), R("PV")], writes=[R("FB", s, oc)], tab="sig")
                    B.op("act", lambda e: e.activation(
                        out=FB[s][:, 4 + oc, :], in_=PS[bi], func=AF.Sigmoid, bias=PV[:, BI + c:BI + c + 1], scale=1.0),
                        reads=[R("PS", bi), R("PV")], writes=[R("FB", s, 4 + oc)], tab="sig")
                rel(pi)
                yield 2.0
                Rr = [FB[s][:, oc, :] for oc in range(2)]
                Mm = [FB[s][:, 2 + oc, :] for oc in range(2)]
                Ii = [FB[s][:, 4 + oc, :] for oc in range(2)]
                rR = [[R("FB", s, oc)] for oc in range(2)]
                rM = [[R("FB", s, 2 + oc)] for oc in range(2)]
                rI = [[R("FB", s, 4 + oc)] for oc in range(2)]
                cs = [2 * hd + oc for oc in range(2)]
                for oc in range(2):
                    B.op("act", lambda e, oc=oc: e.activation(out=Mm[oc], in_=Rr[oc], func=AF.Exp,
                                                              scale=CL[:, 16 + cs[oc]:16 + cs[oc] + 1]),
                         reads=rR[oc] + [R("CL")], writes=rM[oc], tab="exp")
                    B.op("act", lambda e, oc=oc: e.activation(out=Rr[oc], in_=Rr[oc], func=AF.Exp,
                                                              scale=CL[:, 8 + cs[oc]:8 + cs[oc] + 1]),
                         reads=rR[oc] + [R("CL")], writes=rR[oc], tab="exp")
                for oc in range(2):
                    B.op("act", lambda e, oc=oc: e.activation(out=Mm[oc], in_=Mm[oc], func=AF.Sqrt,
                                                              bias=ONEC[:, 0:1], scale=-1.0),
                         reads=rM[oc], writes=rM[oc], tab="sqrt")
                for oc in range(2):
                    B.op("dve", lambda e, oc=oc: e.tensor_tensor(out=Ii[oc], in0=Ii[oc], in1=Mm[oc], op=ALU.mult),
                         reads=rI[oc] + rM[oc], writes=rI[oc], f=2.0)
                for oc in range(2):
                    B.op("dve", lambda e, oc=oc: e.tensor_tensor(out=Ii[oc], in0=Ii[oc], in1=XBv[:, oc, :], op=ALU.mult),
                         reads=rI[oc] + [R("XBc", s, oc)] + bigu(s, 4608 + oc * 512, 5120 + oc * 512),
                         writes=rI[oc], f=2.0)
                yield 4.0
                for oc in range(2):
                    B.op("dve", lambda e, oc=oc: e.tensor_tensor_scan(
                        out=Mm[oc], data0=Rr[oc], data1=Ii[oc], initial=HST[:, cs[oc]:cs[oc] + 1],
                        op0=ALU.mult, op1=ALU.add),
                        reads=rR[oc] + rI[oc] + [R("HST", cs[oc])], writes=rM[oc], f=2.0)
                for oc in range(2):
                    B.op("dve", lambda e, oc=oc: e.tensor_copy(out=HST[:, cs[oc]:cs[oc] + 1], in_=Mm[oc][:, T - 1:T]),
                         reads=rM[oc], writes=[R("HST", cs[oc])], n=1)
                for oc in range(2):
                    B.op("dve", lambda e, oc=oc: e.tensor_tensor(out=HY[s][:, cs[oc], :], in0=Mm[oc], in1=Yv[:, oc, :],
                                                                 op=ALU.mult),
                         reads=rM[oc] + rY, writes=[R("HY", s, cs[oc])], f=2.0)
                yield 4.0
            lru_done[k] = True
            yield from out_proj(s, k, "lout", BO, PV, NM + 8, presq, final)

        SCALE = 0.125
        NB_ = T // 128

        def attn(s, k, presq, final=None):
            BIGb = BIG[s][:, :].bitcast(BF16)
            QZ = BIGb[:, 0:8192].rearrange("p (c h t) -> p c h t", c=8, h=2)
            KT = BIGb[:, 8192:8192 + 128 + T]
            VP = BIGb[:, 8832:8832 + (NB_ + 1) * 256].rearrange("p (b h m) -> p b h m", b=NB_ + 1, h=2)
            rQZ = bigu(s, 0, 4096)
            rKT = bigu(s, 4096, 4416)
            rVP = bigu(s, 4416, 4416 + (NB_ + 1) * 128)
            EEs = [FB[s][:, kk, :].bitcast(BF16).rearrange("p (h q) -> p h q", h=4) for kk in range(3)]
            PTs = [FB[s][:, 3 + kk, :].bitcast(BF16).rearrange("p (a q) -> p a q", a=8) for kk in range(3)]
            rEE = [R("FB", s, kk) for kk in range(3)]
            rPT = [R("FB", s, 3 + kk) for kk in range(3)]
            norm_in(s, NM + 16)
            yield 10.0
            if final is not None:
                early_io(s, final)
            B.op("dve", lambda e: e.memset(QZ[64:128, :, 0, :], 0.0), writes=rQZ, n=4096)
            B.op("dve", lambda e: e.memset(QZ[0:64, :, 1, :], 0.0), writes=rQZ, n=4096)
            B.op("dve", lambda e: e.memset(VP[:, 1:NB_ + 1, 0, 64:128], 0.0), writes=rVP, n=256)
            B.op("dve", lambda e: e.memset(VP[:, 1:NB_ + 1, 1, 0:64], 0.0), writes=rVP, n=256)
            yield 9.0
            while k > 0 and not akv_done[k - 1]:
                yield None
            B.op("dve", lambda e: e.tensor_copy(out=KT[:, 0:128], in_=KC[:, :]), reads=[R("KC")], writes=rKT)
            B.op("dve", lambda e: e.tensor_copy(out=VP[:, 0, :, :], in_=VC[:, :, :]), reads=[R("VC")], writes=rVP)
            pi = acq(("akv", k))
            slot = ws(pi)
            b = nb(s)
            B.mm_group(PS[b], [(W[:, slot, kc * 128:(kc + 1) * 128], XN[s][:, kc, :]) for kc in range(NCH)],
                       reads=[R("W", slot)] + [R("XN", s, kc) for kc in range(NCH)], writes=[R("PS", b)])
            B.op("act", lambda e: e.activation(out=KT[:, 128:128 + T], in_=PS[b], func=AF.Identity,
                                               bias=PV[:, BK:BK + 1], scale=1.0),
                 reads=[R("PS", b), R("PV")], writes=rKT)
            b = nb(s)
            for blk in range(NB_):
                B.mm_group(PS[b][:, blk * 128:(blk + 1) * 128],
                           [(XN[s][:, kc, blk * 128:(blk + 1) * 128], W[:, slot, 1024 + kc * 128:1024 + (kc + 1) * 128])
                            for kc in range(NCH)],
                           reads=[R("W", slot)] + [R("XN", s, kc) for kc in range(NCH)], writes=[R("PS", b)])
            rel(pi)
            PSv = PS[b].rearrange("p (b m) -> p b m", b=NB_)
            B.op("dve", lambda e: e.tensor_tensor(
                out=VP[:, 1:NB_ + 1, 0, 0:64], in0=PSv[:, :, 0:64],
                in1=BC[:, 0:64].unsqueeze(1).broadcast_to([128, NB_, 64]), op=ALU.add),
                reads=[R("PS", b), R("BC")], writes=rVP)
            B.op("dve", lambda e: e.tensor_tensor(
                out=VP[:, 1:NB_ + 1, 1, 64:128], in0=PSv[:, :, 64:128],
                in1=BC[:, 64:128].unsqueeze(1).broadcast_to([128, NB_, 64]), op=ALU.add),
                reads=[R("PS", b), R("BC")], writes=rVP)
            B.op("dve", lambda e: e.tensor_copy(out=KC[:, :], in_=KT[:, T:T + 128]), reads=rKT, writes=[R("KC")])
            B.op("dve", lambda e: e.tensor_copy(out=VC[:, :, :], in_=VP[:, NB_, :, :]), reads=rVP, writes=[R("VC")])
            akv_done[k] = True
            yield 4.0
            for pq in range(4):
                pi = acq(("aq", k, pq))
                slot = ws(pi)
                for c2 in range(2):
                    c = 2 * pq + c2
                    b = nb(s)
                    B.mm_group(PS[b],
                               [(W[:, slot, (c2 * 8 + kc) * 128:(c2 * 8 + kc + 1) * 128], XN[s][:, kc, :])
                                for kc in range(NCH)],
                               reads=[R("W", slot)] + [R("XN", s, kc) for kc in range(NCH)], writes=[R("PS", b)])
                    B.op("act", lambda e: e.activation(
                        out=QZ[0:64, c, 0, :], in_=PS[b][0:64, :], func=AF.Identity,
                        bias=PV[0:64, BQ + c:BQ + c + 1], scale=1.0),
                        reads=[R("PS", b), R("PV")], writes=bigu(s, c * 512, (c + 1) * 512))
                    B.op("act", lambda e: e.activation(
                        out=QZ[64:128, c, 1, :], in_=PS[b][64:128, :], func=AF.Identity,
                        bias=PV[64:128, BQ + c:BQ + c + 1], scale=1.0),
                        reads=[R("PS", b), R("PV")], writes=bigu(s, c * 512, (c + 1) * 512))
                rel(pi)
                yield 3.8

            steps = [(n, g) for n in range(NB_) for g in range(4)]
            NSTEP = len(steps)

            def phaseA(i):
                n, g = steps[i]
                kk = i % 3
                b = nb2(s)
                mk = MASK0 if (k == 0 and n == 0) else MASK
                for hp in range(2):
                    c = 2 * g + hp
                    for hf in range(2):
                        B.mm_group(PSALL[:, (b + hp) * 512 + hf * 256:(b + hp) * 512 + (hf + 1) * 256],
                                   [(QZ[:, c, hf, n * 128:(n + 1) * 128], KT[:, n * 128:n * 128 + 256]),
                                    (IDENT[:, :], mk[:, :])],
                                   reads=bigu(s, c * 512, (c + 1) * 512) + rKT + [R("IDENT"), R("MASK"), R("MASK0")],
                                   writes=[R("PS", b + hp)])
                for hp in range(2):
                    bank = b + hp
                    mxc = 0 if hp == 0 else 14
                    ngc = 1 if hp == 0 else 15
                    B.op("dve", lambda e, bank=bank, mxc=mxc: e.tensor_reduce(
                        out=ST[s][:, kk, mxc:mxc + 1], in_=PS[bank], axis=AX.X, op=ALU.max),
                        reads=[R("PS", bank)], writes=[R("ST", s, kk, 0, hp)], n=512)
                    B.op("dve", lambda e, mxc=mxc, ngc=ngc: e.tensor_scalar(
                        out=ST[s][:, kk, ngc:ngc + 1], in0=ST[s][:, kk, mxc:mxc + 1], scalar1=-SCALE,
                        scalar2=NSMAX[:, 0:1], op0=ALU.mult, op1=ALU.min),
                        reads=[R("ST", s, kk, 0, hp), R("NSMAX")], writes=[R("ST", s, kk, 1, hp)], n=4)
                    for hf in range(2):
                        hh = 2 * hp + hf
                        B.op("act", lambda e, bank=bank, hf=hf, hh=hh, ngc=ngc: e.activation(
                            out=EEs[kk][:, hh, :], in_=PSALL[:, bank * 512 + hf * 256:bank * 512 + (hf + 1) * 256],
                            func=AF.Exp, bias=ST[s][:, kk, ngc:ngc + 1], scale=SCALE,
                            accum_out=ST[s][:, kk, 2 + hh:3 + hh]),
                            reads=[R("PS", bank), R("ST", s, kk, 1, hp)],
                            writes=[R("EE", s, kk, hh), R("ST", s, kk, 2, hh)] + ([rEE[kk]] if hh == 0 else []),
                            n=400, tab="exp")
                    B.op("act", lambda e, hp=hp, ngc=ngc: e.activation(
                        out=ST[s][:, kk, 6 + 2 * hp:8 + 2 * hp], in_=BC[:, 128 + 4 * g + 2 * hp:128 + 4 * g + 2 * hp + 2],
                        func=AF.Exp, bias=ST[s][:, kk, ngc:ngc + 1], scale=1.0),
                        reads=[R("ST", s, kk, 1, hp), R("BC")], writes=[R("ST", s, kk, 3, hp)], n=4, tab="exp")

            def phaseB(i):
                n, g = steps[i]
                kk = i % 3
                B.op("dve", lambda e: e.tensor_tensor(out=ST[s][:, kk, 10:14], in0=ST[s][:, kk, 2:6],
                                                      in1=ST[s][:, kk, 6:10], op=ALU.add),
                     reads=[R("ST", s, kk, 2, hh) for hh in range(4)] + [R("ST", s, kk, 3, hp) for hp in range(2)],
                     writes=[R("ST", s, kk, 4)], n=4)
                B.op("dve", lambda e: e.reciprocal(out=ST[s][:, kk, 10:14], in_=ST[s][:, kk, 10:14]),
                     reads=[R("ST", s, kk, 4)], writes=[R("ST", s, kk, 4)], n=32)
                for hh in range(4):
                    B.op("dve", lambda e, hh=hh: e.tensor_scalar(
                        out=EEs[kk][:, hh, :], in0=EEs[kk][:, hh, :], scalar1=ST[s][:, kk, 10 + hh:11 + hh],
                        scalar2=None, op0=ALU.mult),
                        reads=[R("EE", s, kk, hh), R("ST", s, kk, 4)], writes=[R("EE", s, kk, hh)], n=256)

            def phaseB2(i):
                n, g = steps[i]
                kk = i % 3
                bt = nb(s)
                PTb = PS[bt].bitcast(BF16)
                def emit_tr():
                    ins = None
                    for hh in range(4):
                        for k2 in range(2):
                            a_ = hh * 2 + k2
                            ins = nc.tensor.transpose(PTb[:, a_ * 128:(a_ + 1) * 128],
                                                      EEs[kk][:, hh, k2 * 128:(k2 + 1) * 128], IDENT[:, :])
                    return ins
                B.pe_manual(emit_tr, [rEE[kk], R("IDENT")] + [R("EE", s, kk, hh) for hh in range(4)],
                            [R("PS", bt)], 8 * 0.08)
                B.op("act", lambda e: e.activation(out=PTs[kk], in_=PTb.rearrange("p (a q) -> p a q", a=8),
                                                   func=AF.Copy),
                     reads=[R("PS", bt)], writes=[rPT[kk]], n=1024)

            def phaseC(i):
                n, g = steps[i]
                kk = i % 3
                bo = nb(s)
                for ci in range(2):
                    B.mm_group(PS[bo][:, ci * 128:(ci + 1) * 128],
                               [(VP[:, n + k2, hf, :], PTs[kk][:, (ci * 2 + hf) * 2 + k2, :])
                                for hf in range(2) for k2 in range(2)],
                               reads=rVP + [rPT[kk]], writes=[R("PS", bo)])
                B.op("act", lambda e: e.activation(
                    out=HY[s][:, 2 * g:2 * g + 2, n * 128:(n + 1) * 128],
                    in_=PS[bo][:, 0:256].rearrange("p (c q) -> p c q", c=2), func=AF.Copy),
                    reads=[R("PS", bo)], writes=[R("HY", s, 2 * g), R("HY", s, 2 * g + 1)], n=256)

            for i in range(NSTEP + 4):
                if i < NSTEP:
                    phaseA(i)
                if 0 <= i - 1 < NSTEP:
                    phaseB(i - 1)
                if 0 <= i - 2 < NSTEP:
                    phaseB2(i - 2)
                if 0 <= i - 4 < NSTEP:
                    phaseC(i - 4)
                if i % ay == ay - 1:
                    yield 3.6
            yield from out_proj(s, k, "ao", AO, PV, NM + 24, presq, final)

        def chain(s):
            for k in range(s, ntiles, NS_):
                if k < NS_:
                    for i in range(NCH):
                        B.dma("sp", xsem[s][i], X[s][:, i, :], xT3[:, i, k * T:(k + 1) * T],
                              writes=[R("X", s, i)], nbytes=1 << 18)
                yield 1.0
                sq_ready[s] = False
                for si, st in enumerate(stages):
                    last = si + 1 == len(stages)
                    presq = not last
                    final = k if last else None
                    if st.startswith("ffn"):
                        yield from ffn(s, k, int(st[3]), presq, final)
                    elif st == "lru":
                        yield from lru(s, k, presq, final)
                    elif st == "attn":
                        yield from attn(s, k, presq, final)
                yield 1.0

        issue()
        gens = [chain(s) for s in range(NS_)]
        start_off = [0.0, offset]
        alive = [True, True]
        while any(alive):
            order = sorted([s for s in range(NS_) if alive[s]], key=lambda s: max(start_off[s], B.stime[s]))
            progressed = False
            for s in order:
                B.cur = s
                try:
                    r = next(gens[s])
                except StopIteration:
                    alive[s] = False
                    progressed = True
                    break
                if r is None:
                    continue
                progressed = True
                break
            assert progressed, "scheduler deadlock"
        for s in range(NS_):
            for i in range(NCH):
                B._wait("sp", osem[s][i], osem[s][i].v)
                B._wait("sp", asem[s][i], asem[s][i].v)
    return nc


def _tile_kn(w):
    k, n = w.shape
    return w.reshape(k // 128, 128, n).transpose(1, 0, 2)


def _cols(v):
    return np.ascontiguousarray(v.reshape(-1, 128).T)


def prepare_shared(inp):
    f = np.float32
    g = {k: np.asarray(v, dtype=f) for k, v in inp.items() if k != "x"}
    wgu = np.empty((4, NFF, 128, 2048), f)
    wdn = np.empty((4, 16, 128, 1408), f)
    for l in range(2):
        for w in range(2):
            fi = l * 2 + w
            wg = _tile_kn(g["ffn_w_gate"][l, w]).reshape(128, 8, NFF, 128).transpose(2, 0, 1, 3)
            wu = _tile_kn(g["ffn_w_up"][l, w]).reshape(128, 8, NFF, 128).transpose(2, 0, 1, 3)
            wgu[fi, :, :, 0:1024] = wg.reshape(NFF, 128, 1024)
            wgu[fi, :, :, 1024:2048] = wu.reshape(NFF, 128, 1024)
            wd = _tile_kn(g["ffn_w_down"][l, w])
            wd = wd.reshape(128, 2, 11, 8, 128).transpose(3, 1, 0, 2, 4)
            wdn[fi] = wd.reshape(16, 128, 1408)
    wmix = np.zeros((25, 128, 2048), f)
    win = _tile_kn(g["lru_w_in"][0])
    for c in range(8):
        wmix[WM_LIN + c, :, 0:1024] = win[:, :, c * 128:(c + 1) * 128].reshape(128, 1024)
        wmix[WM_LIN + c, :, 1024:2048] = win[:, :, 1024 + c * 128:1024 + (c + 1) * 128].reshape(128, 1024)
    for hd in range(4):
        wmix[WM_LG + hd, :, 0:512] = _tile_kn(g["lru_w_a"][0, hd]).reshape(128, 512)
        wmix[WM_LG + hd, :, 512:1024] = _tile_kn(g["lru_w_i"][0, hd]).reshape(128, 512)
    wout = _tile_kn(g["lru_w_out"][0])
    for ip in range(4):
        for i2 in range(2):
            i = 2 * ip + i2
            wmix[WM_LOUT + ip, :, i2 * 1024:(i2 + 1) * 1024] = wout[:, :, i * 128:(i + 1) * 128].reshape(128, 1024)
    wqkv = g["attn_w_qkv"][0]
    qcols = np.concatenate([np.r_[c * 64:(c + 1) * 64, (8 + c) * 64:(9 + c) * 64] for c in range(8)])
    wq = _tile_kn(wqkv[:, qcols])
    for pq in range(4):
        for c2 in range(2):
            c = 2 * pq + c2
            wmix[WM_AQ + pq, :, c2 * 1024:(c2 + 1) * 1024] = wq[:, :, c * 128:(c + 1) * 128].reshape(128, 1024)
    wmix[WM_AKV, :, 0:1024] = _tile_kn(wqkv[:, 1024:1152]).reshape(128, 1024)
    wmix[WM_AKV, :, 1024:2048] = _tile_kn(wqkv[:, 1152:1280]).reshape(128, 1024)
    wo = _tile_kn(g["attn_w_o"][0][qcols, :])
    for ip in range(4):
        for i2 in range(2):
            i = 2 * ip + i2
            wmix[WM_AO + ip, :, i2 * 1024:(i2 + 1) * 1024] = wo[:, :, i * 128:(i + 1) * 128].reshape(128, 1024)
    pv = np.zeros((128, NPV), f)
    pv[:, NF:NF + 64] = _cols(g["norm_ffn"].reshape(-1))
    pv[:, NM:NM + 32] = _cols(g["norm_mix"].reshape(-1))
    pv[:, BIN:BIN + 16] = _cols(g["lru_b_in"][0])
    pv[:, CW:CW + 32] = _cols(g["lru_conv_w"][0].reshape(-1))
    pv[:, CB:CB + 8] = _cols(g["lru_conv_b"][0])
    pv[:, BA:BA + 8] = _cols(g["lru_b_a"][0])
    pv[:, BI:BI + 8] = _cols(g["lru_b_i"][0])
    pv[:, LAM:LAM + 8] = _cols(g["lru_lambda"][0])
    pv[:, BO:BO + 8] = _cols(g["lru_b_out"][0])
    bqkv = g["attn_b_qkv"][0]
    pv[:, BQ:BQ + 8] = _cols(bqkv[qcols])
    pv[:, BK:BK + 1] = _cols(bqkv[1024:1152])
    pv[:, AO:AO + 8] = _cols(g["attn_b_o"][0])
    bc = np.zeros((128, 144), f)
    bc[:, 0:128] = np.broadcast_to(bqkv[1152:1280], (128, 128))
    horder = np.array([hf * 8 + c for c in range(8) for hf in range(2)])
    bc[:, 128:144] = np.broadcast_to(g["attn_sinks"][0][horder], (128, 16))
    return dict(wgu=wgu.reshape(4 * NFF, 128, 2048), wdn=wdn.reshape(64, 128, 1408), wmix=wmix, pvec=pv, bcast=bc)


_PROG = {}


def kernel(**inputs):
    x = np.asarray(inputs["x"], dtype=np.float32)
    shared = prepare_shared(inputs)
    if "full" not in _PROG:
        _PROG["full"] = build_program()
    nc = _PROG["full"]
    in_maps = []
    for b in range(NCORES):
        m = dict(shared)
        m["xT"] = np.ascontiguousarray(x[b].T)
        in_maps.append(m)
    res = run_bass_kernel_spmd(nc, in_maps, core_ids=list(range(NCORES)))
    out = np.stack([np.ascontiguousarray(r["outT"].T) for r in res.results], axis=0)
    return out.astype(np.float32)
```
